# Optimizing a Trainium2 kernel written in Bass

```python
import math
import jax, jax.numpy as jnp
from jax import lax
import numpy as np

D_MODEL = 1024
BATCH = 16
SEQ = 2048
DEPTH = 4

N_MIXERS = 3
D_INNER = D_MODEL
HEAD_DIM = 128
HGRN_HEADS = D_INNER // HEAD_DIM
HGRN_CHUNK = 64
MOBA_HEADS = D_INNER // HEAD_DIM
MOBA_BLOCK = 256
MOBA_TOPK = 3
MOBA_QCHUNK = 32
ROPE_THETA = 10000.0
S5_GROUP = 16
S5_GROUPS = D_INNER // S5_GROUP
S5_STATE = 64
DEEPNORM_ALPHA = (2 * DEPTH) ** 0.25
DEEPNORM_BETA = (8 * DEPTH) ** -0.25
LN_EPS = 1e-5
RMS_EPS = 1e-6
NEG = -1e30

kernel_name = "hybrid_hgrn2_moba_s5_deepnorm"


def layer_norm(x, g, b):
    xf = x.astype(jnp.float32)
    mu = jnp.mean(xf, axis=-1, keepdims=True)
    var = jnp.mean(jnp.square(xf - mu), axis=-1, keepdims=True)
    y = (xf - mu) * lax.rsqrt(var + LN_EPS) * g.astype(jnp.float32) + b.astype(jnp.float32)
    return y.astype(x.dtype)


def hgrn2_lower_bounds(lb_logits):
    sm = jax.nn.softmax(lb_logits.astype(jnp.float32), axis=0)
    return jnp.cumsum(sm, axis=0) - sm[0:1]


def hgrn2_chunk_scan(q, k, v, logf):
    bsz, s, h, dk = q.shape
    dv = v.shape[-1]
    n = s // HGRN_CHUNK

    def to_chunks(t):
        return t.reshape(bsz, n, HGRN_CHUNK, h, t.shape[-1]).transpose(1, 0, 3, 2, 4)

    causal = jnp.tril(jnp.ones((HGRN_CHUNK, HGRN_CHUNK), dtype=bool))

    def step(state, inp):
        qc, kc, vc, gc = inp
        bcum = jnp.cumsum(gc, axis=2)
        diff = bcum[:, :, :, None, :] - bcum[:, :, None, :, :]
        decay = jnp.exp(jnp.where(causal[:, :, None], diff, -jnp.inf))
        scores = jnp.einsum('bhtk,bhtsk,bhsk->bhts', qc, decay, kc)
        o = jnp.einsum('bhts,bhsv->bhtv', scores, vc) + \
            jnp.einsum('bhtk,bhkv->bhtv', qc * jnp.exp(bcum), state)
        b_last = bcum[:, :, -1:, :]
        state = jnp.exp(b_last[:, :, 0, :])[..., None] * state + \
            jnp.einsum('bhsk,bhsv->bhkv', kc * jnp.exp(b_last - bcum), vc)
        return state, o

    state0 = jnp.zeros((bsz, h, dk, dv), jnp.float32)
    _, o = lax.scan(step, state0, (to_chunks(q), to_chunks(k), to_chunks(v), to_chunks(logf)))
    return o.transpose(1, 0, 3, 2, 4).reshape(bsz, s, h, dv)


def hgrn2_mixer(x, w_in, norm_g, w_out, lb):
    bsz, s, _ = x.shape
    q, zf, v, gate = jnp.split(x @ w_in, 4, axis=-1)
    shp = (bsz, s, HGRN_HEADS, HEAD_DIM)
    q = jax.nn.silu(q.astype(jnp.float32)).reshape(shp)
    v = v.astype(jnp.float32).reshape(shp)
    lbh = lb.reshape(HGRN_HEADS, HEAD_DIM)
    f = lbh + (1.0 - lbh) * jax.nn.sigmoid(zf.astype(jnp.float32).reshape(shp))
    logf = jnp.log(f)
    k = -jnp.expm1(logf)
    o = hgrn2_chunk_scan(q, k, v, logf)
    o = o * lax.rsqrt(jnp.mean(jnp.square(o), -1, keepdims=True) + RMS_EPS) * \
        norm_g.astype(jnp.float32).reshape(HGRN_HEADS, HEAD_DIM)
    o = o.reshape(bsz, s, D_INNER) * jax.nn.silu(gate.astype(jnp.float32))
    return o.astype(x.dtype) @ w_out


def rope(t, cos, sin):
    half = t.shape[-1] // 2
    t1, t2 = t[..., :half], t[..., half:]
    return jnp.concatenate([t1 * cos - t2 * sin, t2 * cos + t1 * sin], axis=-1)


def moba_attention(q, k, v):
    bsz, h, s, d = q.shape
    nb = -(-s // MOBA_BLOCK)
    s_pad = nb * MOBA_BLOCK
    pad = ((0, 0), (0, 0), (0, s_pad - s), (0, 0))
    qb = jnp.pad(q, pad).reshape(bsz, h, nb, MOBA_BLOCK, d)
    kb = jnp.pad(k, pad).reshape(bsz, h, nb, MOBA_BLOCK, d)
    vb = jnp.pad(v, pad).reshape(bsz, h, nb, MOBA_BLOCK, d)

    k_mean = jnp.mean(kb, axis=3)
    gate = jnp.einsum('bhtd,bhnd->bhtn', q, k_mean)
    q_blk = jnp.arange(s) // MOBA_BLOCK
    past = jnp.arange(nb)[None, :] < q_blk[:, None]
    gate = jnp.where(past, gate, -jnp.inf)
    topk = min(MOBA_TOPK, nb)
    top_val, top_idx = lax.top_k(gate, topk)
    sel_valid = jnp.isfinite(top_val)

    nqc = s // MOBA_QCHUNK

    def to_qchunks(t):
        return t.reshape(bsz, h, nqc, MOBA_QCHUNK, t.shape[-1]).transpose(0, 2, 1, 3, 4) \
            .reshape(bsz * nqc, h, MOBA_QCHUNK, t.shape[-1])

    b_ids = jnp.repeat(jnp.arange(bsz), nqc)
    h_ids = jnp.arange(h)[:, None, None]

    def past_chunk(args):
        qc, ic, mc, bi = args
        kg = kb[bi][h_ids, ic]
        vg = vb[bi][h_ids, ic]
        sc = jnp.einsum('hqd,hqjsd->hqjs', qc, kg)
        sc = jnp.where(mc[..., None], sc, NEG)
        m = jnp.max(sc, axis=(2, 3))
        p = jnp.exp(sc - m[..., None, None]) * mc[..., None]
        return m, jnp.sum(p, axis=(2, 3)), jnp.einsum('hqjs,hqjsd->hqd', p, vg)

    m_p, l_p, acc_p = lax.map(past_chunk, (to_qchunks(q), to_qchunks(top_idx),
                                           to_qchunks(sel_valid), b_ids))
    m_p = m_p.reshape(bsz, nqc, h, MOBA_QCHUNK).transpose(0, 2, 1, 3).reshape(bsz, h, s)
    l_p = l_p.reshape(bsz, nqc, h, MOBA_QCHUNK).transpose(0, 2, 1, 3).reshape(bsz, h, s)
    acc_p = acc_p.reshape(bsz, nqc, h, MOBA_QCHUNK, d).transpose(0, 2, 1, 3, 4).reshape(bsz, h, s, d)

    causal = jnp.tril(jnp.ones((MOBA_BLOCK, MOBA_BLOCK), dtype=bool))
    s_o = jnp.where(causal, jnp.einsum('bhnqd,bhnsd->bhnqs', qb, kb), NEG)
    m_o = jnp.max(s_o, axis=-1)
    p_o = jnp.exp(s_o - m_o[..., None])
    l_o = jnp.sum(p_o, axis=-1).reshape(bsz, h, s_pad)[:, :, :s]
    acc_o = jnp.einsum('bhnqs,bhnsd->bhnqd', p_o, vb).reshape(bsz, h, s_pad, d)[:, :, :s]
    m_o = m_o.reshape(bsz, h, s_pad)[:, :, :s]

    m = jnp.maximum(m_p, m_o)
    a_p = jnp.exp(m_p - m)
    a_o = jnp.exp(m_o - m)
    return (acc_p * a_p[..., None] + acc_o * a_o[..., None]) / (l_p * a_p + l_o * a_o)[..., None]


def moba_mixer(x, w_in, w_out):
    bsz, s, _ = x.shape
    q, k, v, gate = jnp.split(x @ w_in, 4, axis=-1)
    shp = (bsz, s, MOBA_HEADS, HEAD_DIM)
    q = q.astype(jnp.float32).reshape(shp)
    k = k.astype(jnp.float32).reshape(shp)
    v = v.astype(jnp.float32).reshape(shp)
    pos = jnp.arange(s, dtype=jnp.float32)
    inv_freq = 1.0 / (ROPE_THETA ** (jnp.arange(0, HEAD_DIM, 2, dtype=jnp.float32) / HEAD_DIM))
    ang = pos[:, None] * inv_freq[None, :]
    cos, sin = jnp.cos(ang)[:, None, :], jnp.sin(ang)[:, None, :]
    q = rope(q, cos, sin) * (HEAD_DIM ** -0.5)
    k = rope(k, cos, sin)
    o = moba_attention(q.transpose(0, 2, 1, 3), k.transpose(0, 2, 1, 3), v.transpose(0, 2, 1, 3))
    o = o.transpose(0, 2, 1, 3).reshape(bsz, s, D_INNER) * jax.nn.silu(gate.astype(jnp.float32))
    return o.astype(x.dtype) @ w_out


def s5_mixer(x, w_in, a_re, a_im, log_dt, b_re, b_im, c_re, c_im, d_skip, w_glu, b_glu, w_out):
    bsz, s, _ = x.shape
    u, gate = jnp.split(x @ w_in, 2, axis=-1)
    u32 = u.astype(jnp.float32)
    ar, ai = a_re.astype(jnp.float32), a_im.astype(jnp.float32)
    dt = jnp.exp(log_dt.astype(jnp.float32))[:, None]
    mag = jnp.exp(dt * ar)
    abar_re, abar_im = mag * jnp.cos(dt * ai), mag * jnp.sin(dt * ai)
    nr, ni = abar_re - 1.0, abar_im
    den = ar * ar + ai * ai
    z_re = (nr * ar + ni * ai) / den
    z_im = (ni * ar - nr * ai) / den
    br, bi = b_re.astype(jnp.float32), b_im.astype(jnp.float32)
    bbar_re = z_re[..., None] * br - z_im[..., None] * bi
    bbar_im = z_re[..., None] * bi + z_im[..., None] * br
    cr, ci = c_re.astype(jnp.float32), c_im.astype(jnp.float32)

    def combine(e1, e2):
        a1r, a1i, b1r, b1i = e1
        a2r, a2i, b2r, b2i = e2
        return (a2r * a1r - a2i * a1i, a2r * a1i + a2i * a1r,
                a2r * b1r - a2i * b1i + b2r, a2r * b1i + a2i * b1r + b2i)

    def scan_one(u_b):
        bu_re = jnp.einsum('gph,sgh->sgp', bbar_re, u_b)
        bu_im = jnp.einsum('gph,sgh->sgp', bbar_im, u_b)
        ar_s = jnp.broadcast_to(abar_re, bu_re.shape)
        ai_s = jnp.broadcast_to(abar_im, bu_re.shape)
        _, _, hr, hi = lax.associative_scan(combine, (ar_s, ai_s, bu_re, bu_im), axis=0)
        return jnp.einsum('ghp,sgp->sgh', cr, hr) - jnp.einsum('ghp,sgp->sgh', ci, hi)

    y = lax.map(scan_one, u32.reshape(bsz, s, S5_GROUPS, S5_GROUP)).reshape(bsz, s, D_INNER)
    y = y + d_skip.astype(jnp.float32) * u32
    y = jax.nn.gelu(y)
    y = y * jax.nn.sigmoid(y @ w_glu.astype(jnp.float32) + b_glu.astype(jnp.float32))
    y = y * jax.nn.silu(gate.astype(jnp.float32))
    return y.astype(x.dtype) @ w_out


def setup_inputs(seed: int = 0) -> dict:
    key = jax.random.key(seed)
    keys = iter(jax.random.split(key, 64))
    f32 = jnp.float32

    def nrm(shape, scale):
        return jax.random.normal(next(keys), shape, f32) * scale

    inp = {}
    inp["x"] = nrm((BATCH, SEQ, D_MODEL), 1.0)
    inp["hgrn_lower_bounds"] = nrm((DEPTH, D_INNER), 0.1)
    w_out_scale = DEEPNORM_BETA * D_INNER ** -0.5
    for i in range(DEPTH):
        kind = i % N_MIXERS
        p = f"l{i}_"
        if kind == 0:
            inp[p + "w_in"] = nrm((D_MODEL, 4 * D_INNER), D_MODEL ** -0.5)
            inp[p + "norm_g"] = 1.0 + nrm((D_INNER,), 0.02)
            inp[p + "w_out"] = nrm((D_INNER, D_MODEL), w_out_scale)
        elif kind == 1:
            inp[p + "w_in"] = nrm((D_MODEL, 4 * D_INNER), D_MODEL ** -0.5)
            inp[p + "w_out"] = nrm((D_INNER, D_MODEL), w_out_scale)
        else:
            inp[p + "w_in"] = nrm((D_MODEL, 2 * D_INNER), D_MODEL ** -0.5)
            inp[p + "a_re"] = -0.5 + nrm((S5_GROUPS, S5_STATE), 0.01)
            inp[p + "a_im"] = math.pi * jnp.broadcast_to(jnp.arange(S5_STATE, dtype=f32), (S5_GROUPS, S5_STATE)) \
                + nrm((S5_GROUPS, S5_STATE), 0.01)
            inp[p + "log_dt"] = jax.random.uniform(next(keys), (S5_GROUPS,), f32,
                                                   minval=math.log(1e-3), maxval=math.log(1e-1))
            inp[p + "b_re"] = nrm((S5_GROUPS, S5_STATE, S5_GROUP), (2 * S5_GROUP) ** -0.5)
            inp[p + "b_im"] = nrm((S5_GROUPS, S5_STATE, S5_GROUP), (2 * S5_GROUP) ** -0.5)
            inp[p + "c_re"] = nrm((S5_GROUPS, S5_GROUP, S5_STATE), (2 * S5_STATE) ** -0.5)
            inp[p + "c_im"] = nrm((S5_GROUPS, S5_GROUP, S5_STATE), (2 * S5_STATE) ** -0.5)
            inp[p + "d"] = nrm((D_INNER,), 1.0)
            inp[p + "w_glu"] = nrm((D_INNER, D_INNER), D_INNER ** -0.5)
            inp[p + "b_glu"] = nrm((D_INNER,), 0.01)
            inp[p + "w_out"] = nrm((D_INNER, D_MODEL), w_out_scale)
        inp[p + "ln_g"] = 1.0 + nrm((D_MODEL,), 0.02)
        inp[p + "ln_b"] = nrm((D_MODEL,), 0.01)
    return inp


def reference(x, hgrn_lower_bounds,
              l0_w_in, l0_norm_g, l0_w_out, l0_ln_g, l0_ln_b,
              l1_w_in, l1_w_out, l1_ln_g, l1_ln_b,
              l2_w_in, l2_a_re, l2_a_im, l2_log_dt, l2_b_re, l2_b_im, l2_c_re, l2_c_im,
              l2_d, l2_w_glu, l2_b_glu, l2_w_out, l2_ln_g, l2_ln_b,
              l3_w_in, l3_norm_g, l3_w_out, l3_ln_g, l3_ln_b):
    lbs = hgrn2_lower_bounds(hgrn_lower_bounds)
    layers = [
        dict(w_in=l0_w_in, norm_g=l0_norm_g, w_out=l0_w_out, ln_g=l0_ln_g, ln_b=l0_ln_b),
        dict(w_in=l1_w_in, w_out=l1_w_out, ln_g=l1_ln_g, ln_b=l1_ln_b),
        dict(w_in=l2_w_in, a_re=l2_a_re, a_im=l2_a_im, log_dt=l2_log_dt, b_re=l2_b_re, b_im=l2_b_im,
             c_re=l2_c_re, c_im=l2_c_im, d=l2_d, w_glu=l2_w_glu, b_glu=l2_b_glu, w_out=l2_w_out,
             ln_g=l2_ln_g, ln_b=l2_ln_b),
        dict(w_in=l3_w_in, norm_g=l3_norm_g, w_out=l3_w_out, ln_g=l3_ln_g, ln_b=l3_ln_b),
    ]
    for i in range(DEPTH):
        p = layers[i]
        kind = i % N_MIXERS
        if kind == 0:
            y = hgrn2_mixer(x, p["w_in"], p["norm_g"], p["w_out"], lbs[i])
        elif kind == 1:
            y = moba_mixer(x, p["w_in"], p["w_out"])
        else:
            y = s5_mixer(x, p["w_in"], p["a_re"], p["a_im"], p["log_dt"], p["b_re"], p["b_im"],
                         p["c_re"], p["c_im"], p["d"], p["w_glu"], p["b_glu"], p["w_out"])
        x = layer_norm(DEEPNORM_ALPHA * x + y, p["ln_g"], p["ln_b"])
    return x
```

```python
import math
import os
CUT = int(os.environ.get('KCUT', '0'))
import numpy as np
from contextlib import ExitStack
import concourse.bass as bass
import concourse.mybir as mybir
from concourse.bass_utils import run_bass_kernel_spmd

F32 = mybir.dt.float32
BF16 = mybir.dt.bfloat16
I32 = mybir.dt.int32
AF = mybir.ActivationFunctionType
ALU = mybir.AluOpType
AX = mybir.AxisListType

D = 1024
S = 2048
NH = 8
HD = 128
KT = 8
NT = S // 128
DEPTH = 4
ALPHA = (2 * DEPTH) ** 0.25
LN_EPS = 1e-5
RMS_EPS = 1e-6
CH = 64
N_CORES = 8
NO_SELF_SYNC = ('pe',)


class KB:
    def __init__(self, nc, es):
        self.nc, self.es = nc, es
        self.eng = {'pe': nc.tensor, 'dve': nc.vector, 'act': nc.scalar, 'pool': nc.gpsimd, 'sp': nc.sync}
        self.semh = {}
        self.cnt = {}
        for e in ('pe', 'dve', 'act', 'pool'):
            self.semh['s_' + e] = es.enter_context(nc.semaphore('s_' + e))
            self.cnt[e] = 0
        self.waited = {e: {} for e in self.eng}
        self.bufs = {}
        self.dsem = {}
        self.nins = 0

    def sb(self, name, shape, dt, es=None):
        self.nalloc = getattr(self, 'nalloc', 0) + 1
        return (es or self.es).enter_context(self.nc.sbuf_tensor("%s_%d" % (name, self.nalloc), shape, dt))

    def barrier(self):
        evs = [('s_' + e, self.cnt[e]) for e in ('pe', 'dve', 'act', 'pool')]
        evs += [(d[0], d[1]) for d in self.dsem.values()]
        for eng in ('pe', 'dve', 'act', 'pool', 'sp'):
            for n, v in evs:
                if v == 0 or self.waited[eng].get(n, 0) >= v:
                    continue
                if n == 's_' + eng and eng in NO_SELF_SYNC:
                    continue
                self.eng[eng].wait_ge(self.semh[n], v)
                self.waited[eng][n] = v

    def ps(self, name, shape, dt):
        return self.es.enter_context(self.nc.psum_tensor(name, shape, dt))

    def _deps(self, eng, reads, writes):
        need = {}

        def add(ev):
            if ev is not None and need.get(ev[0], 0) < ev[1]:
                need[ev[0]] = ev[1]
        for k in reads:
            b = self.bufs.get(k)
            if b:
                add(b[0])
        for k in writes:
            b = self.bufs.get(k)
            if b:
                add(b[0])
                for n, v in b[1].items():
                    add((n, v))
        own = 's_' + eng
        for n, v in need.items():
            if n == own and eng in NO_SELF_SYNC:
                continue
            if self.waited[eng].get(n, 0) >= v:
                continue
            self.eng[eng].wait_ge(self.semh[n], v)
            self.waited[eng][n] = v

    def _record(self, ev, reads, writes):
        for k in reads:
            b = self.bufs.setdefault(k, [None, {}])
            if b[1].get(ev[0], 0) < ev[1]:
                b[1][ev[0]] = ev[1]
        for k in writes:
            self.bufs[k] = [ev, {}]

    def op(self, eng, reads, writes, fn):
        px = [k for k in reads if isinstance(k, str) and k.startswith('pb') and eng != 'pe']
        if px:
            writes = list(writes) + px
        self._deps(eng, reads, writes)
        ins = fn(self.eng[eng])
        self.cnt[eng] += 1
        ins.then_inc(self.semh['s_' + eng], 1)
        self._record(('s_' + eng, self.cnt[eng]), reads, writes)
        self.nins += 1

    def dma(self, q, out, in_, reads, writes, key, **kw):
        self._deps(q, reads, writes)
        if key not in self.dsem:
            nm = 'd%d' % len(self.dsem)
            self.semh[nm] = self.es.enter_context(self.nc.semaphore(nm))
            self.dsem[key] = [nm, 0]
        d = self.dsem[key]
        d[1] += 16
        self.eng[q].dma_start(out=out, in_=in_, **kw).then_inc(self.semh[d[0]], 16)
        self._record((d[0], d[1]), reads, writes)
        self.nins += 1

    def final_wait(self, q, keys):
        for k in keys:
            d = self.dsem[k]
            self.eng[q].wait_ge(self.semh[d[0]], d[1])


LAYER_KINDS = [0, 1, 2, 0]


def declare_inputs(nc, nseq):
    a = {}

    def inp(name, shape):
        a[name] = nc.dram_tensor(name, list(shape), F32, kind="ExternalInput").ap()
    inp("x", (nseq, S, D))
    inp("hgrn_lower_bounds", (DEPTH, D))
    for i, kind in enumerate(LAYER_KINDS):
        p = "l%d_" % i
        if kind == 0:
            inp(p + "w_in", (D, 4 * D)); inp(p + "norm_g", (D,)); inp(p + "w_out", (D, D))
        elif kind == 1:
            inp(p + "w_in", (D, 4 * D)); inp(p + "w_out", (D, D))
        else:
            inp(p + "w_in", (D, 2 * D))
            inp(p + "a_re", (64, 64)); inp(p + "a_im", (64, 64)); inp(p + "log_dt", (64,))
            inp(p + "b_re", (64, 64, 16)); inp(p + "b_im", (64, 64, 16))
            inp(p + "c_re", (64, 16, 64)); inp(p + "c_im", (64, 16, 64))
            inp(p + "d", (D,)); inp(p + "w_glu", (D, D)); inp(p + "b_glu", (D,)); inp(p + "w_out", (D, D))
        inp(p + "ln_g", (D,)); inp(p + "ln_b", (D,))
    return a


def build_program(nseq, layers=(0, 1, 2, 3), dbg=False):
    nc = bass.Bass("TRN2", target_bir_lowering=False)
    A = declare_inputs(nc, nseq)
    out = nc.dram_tensor("out", [nseq, S, D], F32, kind="ExternalOutput").ap()
    emit(nc, A, out, nseq, layers)
    return nc


def emit(nc, A, out, nseq, layers):
    with ExitStack() as es:
        kb = KB(nc, es)
        P = Prog(kb, A, out, nseq, layers)
        P.run()


class Prog:
    def __init__(self, kb, A, out, nseq, layers):
        self.kb, self.A, self.out, self.nseq, self.layers = kb, A, out, nseq, layers
        self.nc = kb.nc
        kb_ = kb
        self.xres = kb_.sb("xres", [128, NT, D], F32)
        self.xT = kb_.sb("xT", [128, KT, S], BF16)
        self.oT = kb_.sb("oT", [128, KT, S], BF16)
        self.wst1 = kb_.sb("wst0", [128, KT, 256], F32)
        self.wst = [self.wst1, self.wst1]
        self.wh = [kb_.sb("wh%d" % i, [128, KT, 512], BF16) for i in range(2)]
        self.ident = kb_.sb("ident", [128, 128], BF16)
        self.identf = kb_.sb("identf", [128, 128], F32)
        self.zb = kb_.sb("zb", [128, D], BF16)
        self.junk = self.zb
        self.st = kb_.sb("stats", [128, 16], F32)
        self.pb = [kb_.ps("pb%d" % i, [128, 512], F32) for i in range(8)]
        self.wslot = 0

    def consts(self):
        kb = self.kb
        it = kb.sb("iota_t", [128, 128], F32)
        kb.op('pool', [], ['iota_t'], lambda e: e.iota(it[:], [[1, 128]], base=0, channel_multiplier=-1,
                                                       allow_small_or_imprecise_dtypes=True))
        kb.op('dve', ['iota_t'], ['identf'], lambda e: e.tensor_single_scalar(self.identf[:], it[:], 0.0, ALU.is_equal))
        kb.op('dve', ['identf'], ['ident'], lambda e: e.tensor_copy(self.ident[:], self.identf[:]))
        self.iota_t = it

    def load_w(self, src_ap, ncols, key=None):
        kb = self.kb
        i = self.wslot
        self.wslot ^= 1
        st, wh = self.wst1, self.wh[i]
        for r in range(2):
            kb.dma('sp', st[:, :, :], src_ap[:, r * 256:(r + 1) * 256].rearrange("(kt p) c -> p kt c", p=128),
                   [], ['wst0'], 'wst0')
            kb.op('pool', ['wst0'], ['wh%d' % i], lambda e: e.tensor_copy(wh[:, :, r * 256:(r + 1) * 256], st[:, :, :]))
        return wh, 'wh%d' % i

    def load_head_w(self, w_in, h, nstream):
        kb = self.kb
        i = self.wslot
        self.wslot ^= 1
        st, wh = self.wst1, self.wh[i]
        for r in range(nstream // 2):
            for jj in range(2):
                j = 2 * r + jj
                kb.dma('sp', st[:, :, jj * 128:(jj + 1) * 128],
                       w_in[:, j * D + h * 128: j * D + (h + 1) * 128].rearrange("(kt p) c -> p kt c", p=128),
                       [], ['wst0'], 'wst0')
            kb.op('pool', ['wst0'], ['wh%d' % i], lambda e: e.tensor_copy(wh[:, :, r * 256:(r + 1) * 256], st[:, :, :]))
        return wh, 'wh%d' % i

    def load_x(self, s):
        kb = self.kb
        for tt in range(NT):
            kb.dma('sp', self.xres[:, tt, :], self.A["x"][s, tt * 128:(tt + 1) * 128, :], [], [('xres', tt)], ('xres', tt))

    def bf(self, i):
        return self.pb[i][:].bitcast(BF16)

    def make_xT(self, tt, src_bf):
        kb = self.kb
        for g in range(2):
            pi = 7 - g
            pt = self.pb[pi]
            for k4 in range(4):
                kt = g * 4 + k4
                kb.op('pe', ['zb', 'ident'], ['pb%d' % pi],
                      lambda e: e.matmul(pt[:, k4 * 128:(k4 + 1) * 128], src_bf[:, kt * 128:(kt + 1) * 128], self.ident[:],
                                         start=True, stop=True))
            kb.op('act', ['pb%d' % pi], [('xT', tt)],
                  lambda e: e.copy(self.xT[:, g * 4:(g + 1) * 4, tt * 128:(tt + 1) * 128],
                                   pt[:, :].rearrange("p (k t) -> p k t", k=4)))

    def init_xT(self):
        kb = self.kb
        for tt in range(NT):
            kb.op('dve', [('xres', tt)], ['zb'], lambda e: e.tensor_copy(self.zb[:], self.xres[:, tt, :]))
            self.make_xT(tt, self.zb)

    def outproj_ln(self, li):
        kb = self.kb
        A = self.A
        p = "l%d_" % li
        self.lng = kb.sb("lng", [128, D], F32, self.les)
        self.lnb = kb.sb("lnb", [128, D], F32, self.les)
        kb.dma('sp', self.lng[:], A[p + "ln_g"].partition_broadcast(128), [], ['lng'], 'lng')
        kb.dma('sp', self.lnb[:], A[p + "ln_b"].partition_broadcast(128), [], ['lnb'], 'lnb')
        st = self.st
        for half in range(2):
            hs = slice(half * 512, (half + 1) * 512)
            wh, wk = self.load_w(A[p + "w_out"][:, hs], 512, None)
            for tt in range(NT):
                ts = slice(tt * 128, (tt + 1) * 128)
                pi = 5 + (tt % 2)
                pbk = self.pb[pi]
                for kt in range(KT):
                    kb.op('pe', [('oT', tt), wk], ['pb%d' % pi],
                          lambda e: e.matmul(pbk[:], self.oT[:, kt, ts], wh[:, kt, :],
                                             start=(kt == 0), stop=(kt == KT - 1)))
                kb.op('dve', ['pb%d' % pi, ('xres', tt)], [('xres', tt)],
                      lambda e: e.scalar_tensor_tensor(self.xres[:, tt, hs], self.xres[:, tt, hs], ALPHA, pbk[:],
                                                       ALU.mult, ALU.add))
        for tt in range(NT):
            z = self.xres[:, tt, :]
            kb.op('act', [('xres', tt)], ['zb', 'st_ss'],
                  lambda e: e.activation(self.junk[:], z, AF.Square, accum_out=st[:, 0:1]))
            kb.op('dve', [('xres', tt)], ['st_sum'], lambda e: e.tensor_reduce(st[:, 1:2], z, AX.X, ALU.add))
            kb.op('dve', ['st_sum'], ['st_mean'], lambda e: e.tensor_scalar(st[:, 2:3], st[:, 1:2], 1.0 / D, None, ALU.mult))
            kb.op('dve', ['st_mean'], ['st_m2'], lambda e: e.tensor_tensor(st[:, 3:4], st[:, 2:3], st[:, 2:3], ALU.mult))
            kb.op('dve', ['st_ss', 'st_m2'], ['st_var'],
                  lambda e: e.scalar_tensor_tensor(st[:, 4:5], st[:, 0:1], 1.0 / D, st[:, 3:4], ALU.mult, ALU.subtract))
            kb.op('dve', ['st_var'], ['st_var'],
                  lambda e: e.tensor_scalar(st[:, 4:5], st[:, 4:5], LN_EPS, None, ALU.add))
            kb.op('act', ['st_var'], ['st_sd'], lambda e: e.activation(st[:, 6:7], st[:, 4:5], AF.Sqrt))
            kb.op('dve', ['st_sd'], ['st_rstd'], lambda e: e.reciprocal(st[:, 5:6], st[:, 6:7]))
            kb.op('dve', [('xres', tt), 'st_mean', 'st_rstd'], [('xres', tt)],
                  lambda e: e.tensor_scalar(z, z, st[:, 2:3], st[:, 5:6], ALU.subtract, ALU.mult))
            kb.op('dve', [('xres', tt), 'lng'], [('xres', tt)], lambda e: e.tensor_tensor(z, z, self.lng[:], ALU.mult))
            kb.op('dve', [('xres', tt), 'lnb'], [('xres', tt)], lambda e: e.tensor_tensor(z, z, self.lnb[:], ALU.add))
            kb.op('pool', [('xres', tt)], ['zb'], lambda e: e.tensor_copy(self.zb[:], z))
            self.make_xT(tt, self.zb)

    def store_out(self, s):
        kb = self.kb
        for tt in range(NT):
            kb.dma('sp', self.out[s, tt * 128:(tt + 1) * 128, :], self.xres[:, tt, :], [('xres', tt)], [], ('st', tt))

    def run(self):
        kb = self.kb
        self.consts()
        self.setup_layers()
        for s in range(self.nseq):
            self.load_x(s)
            self.init_xT()
            for li in self.layers:
                kind = LAYER_KINDS[li]
                with ExitStack() as les:
                    self.les = les
                    if kind == 0:
                        self.hgrn(li)
                    elif kind == 1:
                        self.moba(li)
                    else:
                        self.s5(li)
                    kb.barrier()
                if CUT == 0 or CUT == 6:
                    with ExitStack() as les:
                        self.les = les
                        self.outproj_ln(li)
                        kb.barrier()
            self.store_out(s)
        kb.final_wait('sp', [('st', tt) for tt in range(NT)])

    def setup_layers(self):
        self.setup_hgrn()
        self.setup_moba()

    def setup_hgrn(self):
        kb = self.kb
        A = self.A
        lbr = kb.sb("lbraw", [128, DEPTH, NH], F32)
        self.lb = kb.sb("lb", [128, DEPTH, NH], F32)
        self.oml = kb.sb("oml", [128, DEPTH, NH], F32)
        sm = kb.sb("lbsum", [128, NH], F32)
        for l in range(DEPTH):
            kb.dma('sp', lbr[:, l, :], A["hgrn_lower_bounds"][l, :].rearrange("(h p) -> p h", p=128), [], ['lbraw'], 'lbraw',
                   allow_slow_non_contiguous=True)
        kb.op('act', ['lbraw'], ['lbraw'], lambda e: e.activation(lbr[:], lbr[:], AF.Exp))
        kb.op('dve', ['lbraw'], ['lbsum'], lambda e: e.tensor_tensor(sm[:], lbr[:, 0, :], lbr[:, 1, :], ALU.add))
        kb.op('dve', ['lbraw', 'lbsum'], ['lbsum'], lambda e: e.tensor_tensor(sm[:], sm[:], lbr[:, 2, :], ALU.add))
        kb.op('dve', ['lbraw', 'lbsum'], ['lbsum'], lambda e: e.tensor_tensor(sm[:], sm[:], lbr[:, 3, :], ALU.add))
        kb.op('dve', ['lbsum'], ['lbsum'], lambda e: e.reciprocal(sm[:], sm[:]))
        for l in range(DEPTH):
            kb.op('dve', ['lbraw', 'lbsum'], ['lbraw'],
                  lambda e: e.tensor_tensor(lbr[:, l, :], lbr[:, l, :], sm[:], ALU.mult))
        kb.op('dve', [], ['lb'], lambda e: e.memset(self.lb[:], 0.0))
        for i in range(1, DEPTH):
            kb.op('dve', ['lb', 'lbraw'], ['lb'],
                  lambda e: e.tensor_tensor(self.lb[:, i, :], self.lb[:, i - 1, :], lbr[:, i, :], ALU.add))
        kb.op('dve', ['lb'], ['oml'], lambda e: e.tensor_scalar(self.oml[:], self.lb[:], -1.0, 1.0, ALU.mult, ALU.add))
        self.amask = kb.sb("amask", [128, 128], F32)
        tmp = kb.sb("amtmp", [128, 128], F32)
        it = self.iota_t
        kb.op('dve', ['iota_t'], ['amask'], lambda e: e.tensor_single_scalar(self.amask[:], it[:], 0.0, ALU.is_ge))
        kb.op('dve', ['amask'], ['amask'], lambda e: e.memset(self.amask[0:64, 64:128], 0.0))

    def alloc_hgrn(self):
        kb = self.kb
        _sb = kb.sb
        kb_sb = lambda n, sh, dt: _sb(n, sh, dt, self.les)
        class _K:
            sb = staticmethod(kb_sb)
        kb = _K
        self.hq = kb.sb("hq", [128, 512], F32)
        self.hf = kb.sb("hf", [128, 512], F32)
        self.hlog = kb.sb("hlog", [128, 512], F32)
        self.hkg = kb.sb("hkg", [128, 512], F32)
        self.hbc = kb.sb("hbc", [128, 512], F32)
        self.heb = kb.sb("heb", [128, 512], F32)
        self.henb = kb.sb("henb", [128, 512], F32)
        self.hqt = kb.sb("hqt", [128, 512], BF16)
        self.hkt = kb.sb("hkt", [128, 512], BF16)
        self.hv = kb.sb("hv", [128, 4, 128], BF16)
        self.hsg = kb.sb("hsg", [128, 4, 128], F32)
        self.hktok = kb.sb("hktok", [128, 4, 128], BF16)
        self.ham = kb.sb("ham", [128, 4, 128], BF16)
        self.hSall = kb.sb("hSall", [128, 9, 128], F32)
        self.hKVs = kb.sb("hKVs", [128, 8, 128], F32)
        self.hSb = kb.sb("hSb", [128, 8, 128], BF16)
        self.hon = kb.sb("hon", [128, 4, 128], BF16)
        self.hng = kb.sb("hng", [128, NH], F32)
        self.hst = kb.sb("hst", [128, 16], F32)

    def hgrn(self, li):
        self.alloc_hgrn()
        kb = self.kb
        self.rmask = kb.sb("rmask", [128, 512], F32, self.les)
        kb.op('dve', [], ['rmask'], lambda e: e.memset(self.rmask[:], 1.0))
        kb.op('dve', ['rmask'], ['rmask'],
              lambda e: e.memset(self.rmask[:].rearrange("p (c j) -> p c j", j=CH)[:, :, 0:1], 0.0))
        A = self.A
        p = "l%d_" % li
        w_in = A[p + "w_in"]
        kb.dma('sp', self.hng[:], A[p + "norm_g"].rearrange("(h p) -> p h", p=128), [], ['hng'], 'hng',
               allow_slow_non_contiguous=True)
        pb = self.pb
        for h in range(NH):
            wh, wk = self.load_head_w(w_in, h, 4)
            if CUT == 1:
                return
            lbc = self.lb[:, li, h:h + 1]
            omc = self.oml[:, li, h:h + 1]
            if CUT == 5 and h == 1:
                return
            kb.op('dve', [], [('hSall', 0)], lambda e: e.memset(self.hSall[:, 0, :], 0.0))
            for blk in range(S // 512):
                t0 = blk * 512
                tts = [blk * 4 + j for j in range(4)]
                xk = [('xT', tt) for tt in tts]
                for kt in range(KT):
                    kb.op('pe', xk + [wk], ['pb0'],
                          lambda e: e.matmul(pb[0][:], wh[:, kt, 0:128], self.xT[:, kt, t0:t0 + 512],
                                             start=(kt == 0), stop=(kt == KT - 1)))
                for kt in range(KT):
                    kb.op('pe', xk + [wk], ['pb1'],
                          lambda e: e.matmul(pb[1][:], wh[:, kt, 128:256], self.xT[:, kt, t0:t0 + 512],
                                             start=(kt == 0), stop=(kt == KT - 1)))
                kb.op('act', ['pb0'], ['hq'], lambda e: e.activation(self.hq[:], pb[0][:], AF.Silu))
                kb.op('act', ['pb1'], ['hf'], lambda e: e.activation(self.hf[:], pb[1][:], AF.Sigmoid))
                kb.op('dve', ['hf', 'lb', 'oml'], ['hf'],
                      lambda e: e.tensor_scalar(self.hf[:], self.hf[:], omc, lbc, ALU.mult, ALU.add))
                kb.op('act', ['hf'], ['hlog'], lambda e: e.activation(self.hlog[:], self.hf[:], AF.Ln))
                kb.op('dve', ['hf'], ['hkg'],
                      lambda e: e.tensor_scalar(self.hkg[:], self.hf[:], -1.0, 1.0, ALU.mult, ALU.add))
                kb.op('dve', ['hlog', 'rmask'], ['hbc'],
                      lambda e: e.tensor_tensor_scan(self.hbc[:], self.rmask[:], self.hlog[:], 0.0, ALU.mult, ALU.add))
                kb.op('act', ['hbc'], ['heb'], lambda e: e.activation(self.heb[:], self.hbc[:], AF.Exp))
                kb.op('act', ['hbc'], ['henb'], lambda e: e.activation(self.henb[:], self.hbc[:], AF.Exp, scale=-1.0))
                kb.op('dve', ['hq', 'heb'], ['hqt'], lambda e: e.tensor_tensor(self.hqt[:], self.hq[:], self.heb[:], ALU.mult))
                kb.op('dve', ['hkg', 'henb'], ['hkt'], lambda e: e.tensor_tensor(self.hkt[:], self.hkg[:], self.henb[:], ALU.mult))
                if CUT == 2:
                    return
                for j in range(4):
                    tt = tts[j]
                    ts = slice(tt * 128, (tt + 1) * 128)
                    ls = slice(j * 128, (j + 1) * 128)
                    for kt in range(KT):
                        kb.op('pe', [('xT', tt), wk], ['pb2'],
                              lambda e: e.matmul(pb[2][:, 0:256], self.xT[:, kt, ts], wh[:, kt, 256:512],
                                                 start=(kt == 0), stop=(kt == KT - 1)))
                    kb.op('dve', ['pb2'], [('hv', j)], lambda e: e.tensor_copy(self.hv[:, j, :], pb[2][:, 0:128]))
                    kb.op('act', ['pb2'], [('hsg', j)], lambda e: e.activation(self.hsg[:, j, :], pb[2][:, 128:256], AF.Silu))
                    kb.op('pe', ['hkt', 'hqt'], ['pb3'],
                          lambda e: e.matmul(pb[3][:, ls], self.hkt[:, ls], self.hqt[:, ls], start=True, stop=True))
                    kb.op('pe', ['hkt', 'ident'], ['pb6'],
                          lambda e: e.matmul(pb[6][:, ls], self.hkt[:, ls], self.ident[:], start=True, stop=True))
                kb.op('dve', ['pb3', 'amask'], ['ham'],
                      lambda e: e.tensor_tensor(self.ham[:], pb[3][:].rearrange("p (j t) -> p j t", j=4),
                                                self.amask[:].unsqueeze(1).broadcast_to([128, 4, 128]), ALU.mult))
                kb.op('act', ['pb6'], ['hktok'],
                      lambda e: e.copy(self.hktok[:], pb[6][:].rearrange("p (j t) -> p j t", j=4)))
                if CUT == 3:
                    return
                for c in range(8):
                    j, half = c // 2, c % 2
                    rs = slice(half * 64, half * 64 + 64)
                    bank = 4 if half == 0 else 7
                    reg = pb[bank][:, j * 128:(j + 1) * 128]
                    kb.op('pe', ['hktok', ('hv', j)], ['pb%d' % bank],
                          lambda e: e.matmul(reg, self.hktok[rs, j, :], self.hv[rs, j, :], start=True, stop=True))
                if CUT == 41:
                    return
                for c in range(8):
                    bank = 4 if c % 2 == 0 else 7
                    reg = pb[bank][:, (c // 2) * 128:(c // 2 + 1) * 128]
                    ebc = self.heb[:, c * 64 + 63: c * 64 + 64]
                    kb.op('act', ['pb%d' % bank, 'heb'], [('hKVs', c)],
                          lambda e: e.activation(self.hKVs[:, c, :], reg, AF.Copy, scale=ebc))
                if CUT == 42:
                    return
                for c in range(8):
                    ebc = self.heb[:, c * 64 + 63: c * 64 + 64]
                    kb.op('dve', [('hSall', c), ('hKVs', c), 'heb'], [('hSall', c + 1)],
                          lambda e: e.scalar_tensor_tensor(self.hSall[:, c + 1, :], self.hSall[:, c, :], ebc,
                                                           self.hKVs[:, c, :], ALU.mult, ALU.add))
                if CUT == 43:
                    return
                kb.op('act', [('hSall', c) for c in range(8)], ['hSb'],
                      lambda e: e.copy(self.hSb[:], self.hSall[:, 0:8, :]))
                kb.op('dve', [('hSall', 8)], [('hSall', 0)],
                      lambda e: e.tensor_copy(self.hSall[:, 0, :], self.hSall[:, 8, :]))
                if CUT == 4:
                    return
                hst = self.hst
                for j in range(4):
                    ls = slice(j * 128, (j + 1) * 128)
                    kb.op('pe', ['ham', ('hv', j)], ['pb5'],
                          lambda e: e.matmul(pb[5][:, ls], self.ham[:, j, :], self.hv[:, j, :], start=True, stop=False))
                    for half in range(2):
                        c = 2 * j + half
                        cs = slice(j * 128 + half * 64, j * 128 + half * 64 + 64)
                        kb.op('pe', ['hqt', 'hSb'], ['pb5'],
                              lambda e: e.matmul(pb[5][half * 64:half * 64 + 64, ls], self.hqt[:, cs], self.hSb[:, c, :],
                                                 start=False, stop=True))
                for j in range(4):
                    ls = slice(j * 128, (j + 1) * 128)
                    kb.op('act', ['pb5'], ['zb', 'hst0'],
                          lambda e: e.activation(self.junk[:, ls], pb[5][:, ls], AF.Square, accum_out=hst[:, j:j + 1]))
                kb.op('dve', ['hst0'], ['hst1'],
                      lambda e: e.tensor_scalar(hst[:, 4:8], hst[:, 0:4], 1.0 / HD, RMS_EPS, ALU.mult, ALU.add))
                kb.op('act', ['hst1'], ['hst1b'], lambda e: e.activation(hst[:, 8:12], hst[:, 4:8], AF.Sqrt))
                kb.op('dve', ['hst1b'], ['hst2'], lambda e: e.reciprocal(hst[:, 12:16], hst[:, 8:12]))
                for j in range(4):
                    ls = slice(j * 128, (j + 1) * 128)
                    kb.op('dve', ['pb5', 'hst2', ('hsg', j)], [('hon', j)],
                          lambda e: e.scalar_tensor_tensor(self.hon[:, j, :], pb[5][:, ls], hst[:, 12 + j:13 + j],
                                                           self.hsg[:, j, :], ALU.mult, ALU.mult))
                for j in range(4):
                    ls = slice(j * 128, (j + 1) * 128)
                    kb.op('pe', [('hon', j), 'ident'], ['pb6'],
                          lambda e: e.matmul(pb[6][:, ls], self.hon[:, j, :], self.ident[:], start=True, stop=True))
                kb.op('act', ['pb6', 'hng'], [('oT', tt) for tt in tts],
                      lambda e: e.activation(self.oT[:, h, t0:t0 + 512], pb[6][:], AF.Copy, scale=self.hng[:, h:h + 1]))

    def setup_moba(self):
        kb = self.kb
        it = self.iota_t
        self.pswap = kb.sb("pswap", [128, 128], F32)
        kb.op('dve', ['iota_t'], ['pswap'], lambda e: e.tensor_single_scalar(self.pswap[:], it[:], 64.0, ALU.is_equal))
        kb.op('dve', ['iota_t', 'pswap'], ['pswap'],
              lambda e: e.scalar_tensor_tensor(self.pswap[:], it[:], -64.0, self.pswap[:], ALU.is_equal, ALU.add))

    def moba(self, li):
        kb = self.kb
        A = self.A
        les = self.les
        sb = lambda n, sh, dt: kb.sb(n, sh, dt, les)
        pb = self.pb
        p = "l%d_" % li
        w_in = A[p + "w_in"]
        cosT = sb("m_cos", [128, S], F32)
        sinT = sb("m_sin", [128, S], F32)
        mq = sb("m_q", [128, 512], F32)
        mk = sb("m_k", [128, 512], F32)
        t1 = sb("m_t1", [128, 512], F32)
        qf = mq
        qT = sb("m_qT", [128, S], BF16)
        kT = sb("m_kT", [128, S], BF16)
        v1 = sb("m_v1", [128, NT, 132], BF16)
        sg = sb("m_sg", [128, NT, 128], BF16)
        sel = sb("m_sel", [128, NT, 8], F32)
        km = sb("m_km", [128, 8], F32)
        g8 = sb("m_g8", [128, 8], F32)
        mx8 = sb("m_mx8", [128, 8], F32)
        pT = [sb("m_pT%d" % i, [128, 256], BF16) for i in range(2)]
        cm = sb("m_cm", [128, 2, 256], BF16)
        acc = sb("m_acc", [128, 2, 132], F32)
        mon = sb("m_on", [128, 128], BF16)
        sc = sb("m_sc", [128, 8], F32)
        kb.op('pool', [], ['m_sc'], lambda e: e.iota(sc[:, 0:1], [[0, 1]], base=0, channel_multiplier=1,
                                                     allow_small_or_imprecise_dtypes=True))
        kb.op('dve', ['m_sc'], ['m_sc'], lambda e: e.tensor_scalar(sc[:, 5:6], sc[:, 0:1], 64.0, -64.0, ALU.is_ge, ALU.mult))
        kb.op('dve', ['m_sc'], ['m_sc'], lambda e: e.tensor_tensor(sc[:, 1:2], sc[:, 0:1], sc[:, 5:6], ALU.add))
        kb.op('act', ['m_sc'], ['m_sc'],
              lambda e: e.activation(sc[:, 2:3], sc[:, 1:2], AF.Exp, scale=-math.log(10000.0) / 64.0))
        kb.op('dve', ['m_sc'], ['m_sc'],
              lambda e: e.tensor_scalar(sc[:, 3:4], sc[:, 0:1], 64.0, 2.0, ALU.is_ge, ALU.mult))
        kb.op('dve', ['m_sc'], ['m_sc'], lambda e: e.tensor_single_scalar(sc[:, 3:4], sc[:, 3:4], -1.0, ALU.add))
        kb.op('pool', [], ['m_cos'], lambda e: e.iota(cosT[:], [[1, S]], base=0, channel_multiplier=0,
                                                      allow_small_or_imprecise_dtypes=True))
        kb.op('dve', ['m_cos', 'm_sc'], ['m_cos'], lambda e: e.tensor_scalar(cosT[:], cosT[:], sc[:, 2:3], None, ALU.mult))
        TWO_PI = 2.0 * math.pi
        ki_ap = mq[:].bitcast(I32)

        def reduce_into(dst, dk, shift):
            for c4 in range(S // 512):
                cs = slice(c4 * 512, (c4 + 1) * 512)
                kb.op('dve', ['m_cos'], ['m_t1'],
                      lambda e: e.tensor_scalar(t1[:], cosT[:, cs], shift, 1.0 / TWO_PI, ALU.add, ALU.mult))
                kb.op('dve', ['m_t1'], ['m_q'], lambda e: e.tensor_copy(ki_ap, t1[:]))
                kb.op('dve', ['m_q'], ['m_t1'], lambda e: e.tensor_copy(t1[:], ki_ap))
                kb.op('dve', ['m_t1', 'm_cos'], ['m_t1'],
                      lambda e: e.scalar_tensor_tensor(t1[:], t1[:], -TWO_PI, cosT[:, cs], ALU.mult, ALU.add))
                kb.op('dve', ['m_t1'], ['m_t1'], lambda e: e.tensor_single_scalar(t1[:], t1[:], shift, ALU.add))
                kb.op('dve', ['m_t1'], ['m_k'],
                      lambda e: e.tensor_scalar(mk[:], t1[:], math.pi, -TWO_PI, ALU.is_gt, ALU.mult))
                kb.op('dve', ['m_t1', 'm_k'], ['m_t1'], lambda e: e.tensor_tensor(t1[:], t1[:], mk[:], ALU.add))
                kb.op('dve', ['m_t1'], ['m_k'],
                      lambda e: e.tensor_scalar(mk[:], t1[:], -math.pi, TWO_PI, ALU.is_lt, ALU.mult))
                kb.op('dve', ['m_t1', 'm_k'], [dk], lambda e: e.tensor_tensor(dst[:, cs], t1[:], mk[:], ALU.add))
        reduce_into(sinT, 'm_sin', 0.0)
        reduce_into(cosT, 'm_cos', 0.5 * math.pi)
        kb.op('act', ['m_sin'], ['m_sin'], lambda e: e.activation(sinT[:], sinT[:], AF.Sin))
        kb.op('act', ['m_cos'], ['m_cos'], lambda e: e.activation(cosT[:], cosT[:], AF.Sin))
        kb.op('dve', ['m_sin', 'm_sc'], ['m_sin'], lambda e: e.tensor_scalar(sinT[:], sinT[:], sc[:, 3:4], None, ALU.mult))
        for kk in range(2):
            kb.op('pool', [], ['m_t1'], lambda e: e.iota(t1[:, 0:256], [[1, 256]], base=-128 * kk, channel_multiplier=-1,
                                                        allow_small_or_imprecise_dtypes=True))
            kb.op('dve', ['m_t1'], ['m_cm'], lambda e: e.tensor_single_scalar(cm[:, kk, :], t1[:, 0:256], 0.0, ALU.is_ge))
        kb.op('dve', [], ['m_v1'], lambda e: e.memset(v1[:, :, 128:129], 1.0))
        QS = HD ** -0.5
        for h in range(NH):
            wh, wk = self.load_head_w(w_in, h, 4)
            kb.op('dve', [], ['m_km'], lambda e: e.memset(km[:], 0.0))
            for blk in range(S // 512):
                t0 = blk * 512
                tts = [blk * 4 + j for j in range(4)]
                xk = [('xT', tt) for tt in tts]
                bs = slice(t0, t0 + 512)
                for (pi, c0, dst, dk) in ((0, 0, mq, 'm_q'), (1, 128, mk, 'm_k')):
                    for kt in range(KT):
                        kb.op('pe', xk + [wk], ['pb%d' % pi],
                              lambda e: e.matmul(pb[pi][:], wh[:, kt, c0:c0 + 128], self.xT[:, kt, bs],
                                                 start=(kt == 0), stop=(kt == KT - 1)))
                    kb.op('act', ['pb%d' % pi], [dk], lambda e: e.copy(dst[:], pb[pi][:]))
                for (pi, src, sk, scale, dstb, dbk) in ((2, mq, 'm_q', QS, qT, 'm_qT'), (3, mk, 'm_k', 1.0, kT, 'm_kT')):
                    kb.op('pe', [sk, 'pswap'], ['pb%d' % pi],
                          lambda e: e.matmul(pb[pi][:], self.pswap[:], src[:], start=True, stop=True))
                    kb.op('dve', ['pb%d' % pi, 'm_sin'], ['m_t1'],
                          lambda e: e.tensor_tensor(t1[:], pb[pi][:], sinT[:, bs], ALU.mult))
                    kb.op('dve', [sk, 'm_cos'], [sk], lambda e: e.tensor_tensor(src[:], src[:], cosT[:, bs], ALU.mult))
                    if scale != 1.0:
                        kb.op('dve', [sk, 'm_t1'], [sk], lambda e: e.tensor_tensor(src[:], src[:], t1[:], ALU.add))
                        kb.op('dve', [sk], [sk], lambda e: e.tensor_single_scalar(src[:], src[:], scale, ALU.mult))
                        kb.op('act', [sk], [dbk], lambda e: e.copy(dstb[:, bs], src[:]))
                    else:
                        kb.op('dve', [sk, 'm_t1'], [sk], lambda e: e.tensor_tensor(src[:], src[:], t1[:], ALU.add))
                        kb.op('act', [sk], [dbk], lambda e: e.copy(dstb[:, bs], src[:]))
                kb.op('dve', ['m_k'], ['m_km'],
                      lambda e: e.tensor_reduce(km[:, 2 * blk:2 * blk + 2], mk[:].rearrange("p (b t) -> p b t", b=2), AX.X, ALU.add))
                kb.op('dve', ['m_km'], ['m_km'],
                      lambda e: e.tensor_single_scalar(km[:, 2 * blk:2 * blk + 2], km[:, 2 * blk:2 * blk + 2], 1.0 / 256, ALU.mult))
                for j in range(4):
                    tt = tts[j]
                    ts = slice(tt * 128, (tt + 1) * 128)
                    for kt in range(KT):
                        kb.op('pe', [('xT', tt), wk], ['pb4'],
                              lambda e: e.matmul(pb[4][:, 0:256], self.xT[:, kt, ts], wh[:, kt, 256:512],
                                                 start=(kt == 0), stop=(kt == KT - 1)))
                    kb.op('dve', ['pb4'], [('m_v1', tt)], lambda e: e.tensor_copy(v1[:, tt, 0:128], pb[4][:, 0:128]))
                    kb.op('act', ['pb4'], [('m_sg', tt)], lambda e: e.activation(sg[:, tt, :], pb[4][:, 128:256], AF.Silu))
                    m = tt // 2
                    kb.op('pe', ['m_q', 'm_km'], ['pb5'],
                          lambda e: e.matmul(pb[5][:, 0:8], qf[:, j * 128:(j + 1) * 128], km[:, 0:8], start=True, stop=True))
                    kb.op('dve', [], ['m_g8'], lambda e: e.memset(g8[:], -1e30))
                    if m > 0:
                        kb.op('dve', ['pb5'], ['m_g8'], lambda e: e.tensor_copy(g8[:, 0:m], pb[5][:, 0:m]))
                    kb.op('dve', ['m_g8'], ['m_mx8'], lambda e: e.max(mx8[:], g8[:]))
                    kb.op('dve', ['m_mx8'], ['m_mx8'], lambda e: e.tensor_single_scalar(mx8[:, 2:3], mx8[:, 2:3], -1e29, ALU.max))
                    kb.op('dve', ['m_g8', 'm_mx8'], [('m_sel', tt)],
                          lambda e: e.tensor_scalar(sel[:, tt, :], g8[:], mx8[:, 2:3], None, ALU.is_ge))
            pi_s = 0
            for m in range(S // 256):
                qs = slice(m * 256, (m + 1) * 256)
                order = [m] + list(range(m))
                for n in order:
                    for kk in range(2):
                        ks = slice(n * 256 + kk * 128, n * 256 + (kk + 1) * 128)
                        pi = pi_s % 2
                        pi_s += 1
                        kb.op('pe', ['m_kT', 'm_qT'], ['pb%d' % pi],
                              lambda e: e.matmul(pb[pi][:, 0:256], kT[:, ks], qT[:, qs], start=True, stop=True))
                        kb.op('act', ['pb%d' % pi], ['m_pT%d' % pi], lambda e: e.activation(pT[pi][:], pb[pi][:, 0:256], AF.Exp))
                        if n == m:
                            kb.op('dve', ['m_pT%d' % pi, 'm_cm'], ['m_pT%d' % pi],
                                  lambda e: e.tensor_tensor(pT[pi][:], pT[pi][:], cm[:, kk, :], ALU.mult))
                        for qt in range(2):
                            kb.op('pe', ['m_pT%d' % pi, ('m_v1', n * 2 + kk)], ['pb%d' % (2 + qt)],
                                  lambda e: e.matmul(pb[2 + qt][:, 0:129], pT[pi][:, qt * 128:(qt + 1) * 128],
                                                     v1[:, n * 2 + kk, 0:129], start=(kk == 0), stop=(kk == 1)))
                    for qt in range(2):
                        tt = m * 2 + qt
                        if n == m:
                            kb.op('dve', ['pb%d' % (2 + qt)], ['m_acc'], lambda e: e.tensor_copy(acc[:, qt, 0:129], pb[2 + qt][:, 0:129]))
                        else:
                            kb.op('dve', ['pb%d' % (2 + qt), 'm_acc', ('m_sel', tt)], ['m_acc'],
                                  lambda e: e.scalar_tensor_tensor(acc[:, qt, 0:129], pb[2 + qt][:, 0:129],
                                                                   sel[:, tt, n:n + 1], acc[:, qt, 0:129], ALU.mult, ALU.add))
                for qt in range(2):
                    tt = m * 2 + qt
                    kb.op('dve', ['m_acc'], ['m_sc'], lambda e: e.reciprocal(sc[:, 4:5], acc[:, qt, 128:129]))
                    kb.op('dve', ['m_acc', 'm_sc', ('m_sg', tt)], ['m_on'],
                          lambda e: e.scalar_tensor_tensor(mon[:], acc[:, qt, 0:128], sc[:, 4:5], sg[:, tt, :], ALU.mult, ALU.mult))
                    kb.op('pe', ['m_on', 'ident'], ['pb6'],
                          lambda e: e.matmul(pb[6][:, 0:128], mon[:], self.ident[:], start=True, stop=True))
                    kb.op('act', ['pb6'], [('oT', tt)],
                          lambda e: e.copy(self.oT[:, h, tt * 128:(tt + 1) * 128], pb[6][:, 0:128]))

    def s5(self, li):
        kb = self.kb
        A = self.A
        pb = self.pb
        p = "l%d_" % li
        w_in = A[p + "w_in"]
        NQ = 32
        TWO_PI = 2.0 * math.pi
        with ExitStack() as L:
            sbL = lambda n, sh, dt: kb.sb(n, sh, dt, L)
            pw = sbL("s_pw", [128, 50, NQ], F32)
            BT = sbL("s_BT", [128, 8, 2, 128], BF16)
            CT = sbL("s_CT", [128, 8, 2, 128], BF16)
            msk = sbL("s_msk", [128, 8], F32)
            dd = sbL("s_dd", [128, 8], F32)
            bg = sbL("s_bg", [128, 8], F32)
            kb.dma('sp', dd[:], A[p + "d"].rearrange("(c p) -> p c", p=128), [], ['s_dd'], 's_dd', allow_slow_non_contiguous=True)
            kb.dma('sp', bg[:], A[p + "b_glu"].rearrange("(c p) -> p c", p=128), [], ['s_bg'], 's_bg', allow_slow_non_contiguous=True)
            dv = lambda r, w, f: kb.op('dve', r, w, f)
            with ExitStack() as SS:
                sbS = lambda n, sh, dt: kb.sb(n, sh, dt, SS)
                ps = sbS("s_ps", [128, 24, NQ], F32)
                psi = sbS("s_psi", [128, NQ], I32)
                an = sbS("s_an", [64, 2, 64], F32)
                a2 = sbS("s_a2", [64, 2, 128], F32)
                ld = sbS("s_ld", [128, 64], F32)
                braw = sbS("s_braw", [128, 2, NQ, 16], F32)
                bb = sbS("s_bb", [128, 2, NQ, 16], F32)
                tb = sbS("s_tb", [128, NQ, 16], F32)
                cn = sbS("s_cn", [128, 2, 8, 64], F32)
                S4 = sbS("s_S4", [128, 2, 128], F32)
                M2 = sbS("s_M2", [128, 2, 128], F32)
                pif = sbS("s_pif", [128, 1], F32)
                AR, AI, DT, MAG, ANG, SIN, COS, DEN, NR, ZR, ZI, TF, TM, T1, T2 = range(15)
                P = lambda k: ps[:, k, :]
                kb.dma('sp', an[:, 0, :], A[p + "a_re"], [], ['s_an'], 's_an')
                kb.dma('sp', an[:, 1, :], A[p + "a_im"], [], ['s_an'], 's_an')
                kb.dma('sp', ld[:], A[p + "log_dt"].partition_broadcast(128), [], ['s_ld'], 's_ld')
                for ri, nm in ((0, "b_re"), (1, "b_im")):
                    v = A[p + nm].rearrange("(q two) p h -> two p q h", two=2)
                    for g2 in range(2):
                        kb.dma('sp', braw[g2 * 64:(g2 + 1) * 64, ri, :, :], v[g2], [], ['s_braw'], 's_braw')
                for ri, nm in ((0, "c_re"), (1, "c_im")):
                    kb.dma('sp', cn[:, ri, :, :], A[p + nm].rearrange("(c gg) h p -> (gg h) c p", gg=8), [], ['s_cn'], 's_cn')
                kb.op('pool', [], ['s_pif'], lambda e: e.iota(pif[:], [[0, 1]], base=0, channel_multiplier=1,
                                                              allow_small_or_imprecise_dtypes=True))
                dv([], ['s_msk'], lambda e: e.memset(msk[:], 0.0))
                for r in range(4):
                    dv(['s_pif'], ['s_msk'], lambda e: e.tensor_single_scalar(msk[:, 6:7], pif[:], float(32 * r), ALU.is_ge))
                    dv(['s_pif'], ['s_msk'], lambda e: e.tensor_single_scalar(msk[:, 7:8], pif[:], float(32 * r + 32), ALU.is_lt))
                    dv(['s_msk'], ['s_msk'], lambda e: e.tensor_tensor(msk[:, r:r + 1], msk[:, 6:7], msk[:, 7:8], ALU.mult))
                    dv(['s_pif'], ['s_msk'], lambda e: e.tensor_single_scalar(msk[:, 6:7], pif[:], float(32 * r + 16), ALU.is_ge))
                    dv(['s_msk'], ['s_msk'], lambda e: e.tensor_tensor(msk[:, 6:7], msk[:, 6:7], msk[:, 7:8], ALU.mult))
                    dv(['s_msk'], ['s_msk'], lambda e: e.tensor_tensor(msk[:, 5:6], msk[:, 5:6], msk[:, 6:7], ALU.add))
                dv(['s_msk'], ['s_msk'], lambda e: e.tensor_scalar(msk[:, 4:5], msk[:, 5:6], -1.0, 1.0, ALU.mult, ALU.add))
                dv(['s_an'], ['s_a2'], lambda e: e.tensor_copy(a2[:, :, 0:64], an[:]))
                dv(['s_an'], ['s_a2'], lambda e: e.tensor_copy(a2[:, :, 64:128], an[:]))
                for ri in range(2):
                    kb.op('pe', ['s_a2', 'identf'], ['pb0'],
                          lambda e: e.matmul(pb[0][:, ri * 64:(ri + 1) * 64], a2[:, ri, :], self.identf[0:64, 0:64],
                                             start=True, stop=True))
                for ri, dst in ((0, AR), (1, AI)):
                    v = pb[0][:, ri * 64:(ri + 1) * 64].rearrange("l (q two) -> l q two", two=2)
                    for g2 in range(2):
                        hs = slice(g2 * 64, (g2 + 1) * 64)
                        dv(['pb0'], ['s_ps'], lambda e: e.tensor_copy(ps[hs, dst, :], v[hs, :, g2]))
                v = ld[:].rearrange("l (q two) -> l q two", two=2)
                for g2 in range(2):
                    hs = slice(g2 * 64, (g2 + 1) * 64)
                    dv(['s_ld'], ['s_ps'], lambda e: e.tensor_copy(ps[hs, DT, :], v[hs, :, g2]))
                kb.op('act', ['s_ps'], ['s_ps'], lambda e: e.activation(P(DT), P(DT), AF.Exp))
                dv(['s_ps'], ['s_ps'], lambda e: e.tensor_tensor(P(MAG), P(AR), P(DT), ALU.mult))
                kb.op('act', ['s_ps'], ['s_ps'], lambda e: e.activation(P(MAG), P(MAG), AF.Exp))
                dv(['s_ps'], ['s_ps'], lambda e: e.tensor_tensor(P(ANG), P(AI), P(DT), ALU.mult))

                def sin_of(dst, shift):
                    dv(['s_ps'], ['s_ps'], lambda e: e.tensor_scalar(P(TF), P(ANG), shift, 1.0 / TWO_PI, ALU.add, ALU.mult))
                    dv(['s_ps'], ['s_psi'], lambda e: e.tensor_copy(psi[:], P(TF)))
                    dv(['s_psi'], ['s_ps'], lambda e: e.tensor_copy(P(TF), psi[:]))
                    dv(['s_ps'], ['s_ps'], lambda e: e.scalar_tensor_tensor(P(TF), P(TF), -TWO_PI, P(ANG), ALU.mult, ALU.add))
                    dv(['s_ps'], ['s_ps'], lambda e: e.tensor_single_scalar(P(TF), P(TF), shift, ALU.add))
                    dv(['s_ps'], ['s_ps'], lambda e: e.tensor_scalar(P(TM), P(TF), math.pi, -TWO_PI, ALU.is_gt, ALU.mult))
                    dv(['s_ps'], ['s_ps'], lambda e: e.tensor_tensor(P(TF), P(TF), P(TM), ALU.add))
                    dv(['s_ps'], ['s_ps'], lambda e: e.tensor_scalar(P(TM), P(TF), -math.pi, TWO_PI, ALU.is_lt, ALU.mult))
                    dv(['s_ps'], ['s_ps'], lambda e: e.tensor_tensor(P(TF), P(TF), P(TM), ALU.add))
                    kb.op('act', ['s_ps'], ['s_ps'], lambda e: e.activation(P(dst), P(TF), AF.Sin))
                sin_of(SIN, 0.0)
                sin_of(COS, 0.5 * math.pi)
                W = lambda k: pw[:, k, :]
                dv(['s_ps'], ['s_pw'], lambda e: e.tensor_tensor(W(0), P(MAG), P(COS), ALU.mult))
                dv(['s_ps'], ['s_pw'], lambda e: e.tensor_tensor(W(1), P(MAG), P(SIN), ALU.mult))
                dv(['s_pw'], ['s_pw'], lambda e: e.tensor_single_scalar(W(2), W(1), -1.0, ALU.mult))

                def cmul(d, a_, b_):
                    dv(['s_pw'], ['s_pw'], lambda e: e.tensor_tensor(W(48), W(a_), W(b_), ALU.mult))
                    dv(['s_pw'], ['s_pw'], lambda e: e.tensor_tensor(W(49), W(a_ + 1), W(b_ + 1), ALU.mult))
                    dv(['s_pw'], ['s_pw'], lambda e: e.tensor_tensor(W(d), W(48), W(49), ALU.subtract))
                    dv(['s_pw'], ['s_pw'], lambda e: e.tensor_tensor(W(48), W(a_), W(b_ + 1), ALU.mult))
                    dv(['s_pw'], ['s_pw'], lambda e: e.tensor_tensor(W(49), W(a_ + 1), W(b_), ALU.mult))
                    dv(['s_pw'], ['s_pw'], lambda e: e.tensor_tensor(W(d + 1), W(48), W(49), ALU.add))
                    dv(['s_pw'], ['s_pw'], lambda e: e.tensor_single_scalar(W(d + 2), W(d + 1), -1.0, ALU.mult))
                for j in range(1, 8):
                    cmul(3 * j, 3 * (j - 1), 0)
                for k in range(3):
                    dv(['s_pw'], ['s_pw'], lambda e: e.tensor_copy(W(24 + k), W(21 + k)))
                for k in range(1, 8):
                    cmul(24 + 3 * k, 24 + 3 * (k - 1), 24 + 3 * (k - 1))
                dv(['s_ps'], ['s_ps'], lambda e: e.tensor_tensor(P(DEN), P(AR), P(AR), ALU.mult))
                dv(['s_ps'], ['s_ps'], lambda e: e.tensor_tensor(P(T1), P(AI), P(AI), ALU.mult))
                dv(['s_ps'], ['s_ps'], lambda e: e.tensor_tensor(P(DEN), P(DEN), P(T1), ALU.add))
                dv(['s_ps'], ['s_ps'], lambda e: e.reciprocal(P(DEN), P(DEN)))
                dv(['s_pw'], ['s_ps'], lambda e: e.tensor_single_scalar(P(NR), W(0), -1.0, ALU.add))
                dv(['s_ps'], ['s_ps'], lambda e: e.tensor_tensor(P(T1), P(NR), P(AR), ALU.mult))
                dv(['s_ps', 's_pw'], ['s_ps'], lambda e: e.tensor_tensor(P(T2), W(1), P(AI), ALU.mult))
                dv(['s_ps'], ['s_ps'], lambda e: e.tensor_tensor(P(T1), P(T1), P(T2), ALU.add))
                dv(['s_ps'], ['s_ps'], lambda e: e.tensor_tensor(P(ZR), P(T1), P(DEN), ALU.mult))
                dv(['s_ps', 's_pw'], ['s_ps'], lambda e: e.tensor_tensor(P(T1), W(1), P(AR), ALU.mult))
                dv(['s_ps'], ['s_ps'], lambda e: e.tensor_tensor(P(T2), P(NR), P(AI), ALU.mult))
                dv(['s_ps'], ['s_ps'], lambda e: e.tensor_tensor(P(T1), P(T1), P(T2), ALU.subtract))
                dv(['s_ps'], ['s_ps'], lambda e: e.tensor_tensor(P(ZI), P(T1), P(DEN), ALU.mult))
                zb = lambda k: ps[:, k, :].unsqueeze(2).broadcast_to([128, NQ, 16])
                dv(['s_braw', 's_ps'], ['s_bb'], lambda e: e.tensor_tensor(bb[:, 0], braw[:, 0], zb(ZR), ALU.mult))
                dv(['s_braw', 's_ps'], ['s_tb'], lambda e: e.tensor_tensor(tb[:], braw[:, 1], zb(ZI), ALU.mult))
                dv(['s_bb', 's_tb'], ['s_bb'], lambda e: e.tensor_tensor(bb[:, 0], bb[:, 0], tb[:], ALU.subtract))
                dv(['s_braw', 's_ps'], ['s_bb'], lambda e: e.tensor_tensor(bb[:, 1], braw[:, 1], zb(ZR), ALU.mult))
                dv(['s_braw', 's_ps'], ['s_tb'], lambda e: e.tensor_tensor(tb[:], braw[:, 0], zb(ZI), ALU.mult))
                dv(['s_bb', 's_tb'], ['s_bb'], lambda e: e.tensor_tensor(bb[:, 1], bb[:, 1], tb[:], ALU.add))
                dv([], ['s_S4'], lambda e: e.memset(S4[:], 0.0))
                for c in range(8):
                    for ri in range(2):
                        v0 = S4[0:64, ri, :].rearrange("p (r x) -> p r x", x=32)
                        v1 = S4[64:128, ri, :].rearrange("p (r x) -> p r x", x=32)
                        dv(['s_bb'], ['s_S4'], lambda e: e.tensor_copy(v0[:, :, 0:16], bb[0:64, ri, 4 * c:4 * c + 4, :]))
                        dv(['s_bb'], ['s_S4'], lambda e: e.tensor_copy(v1[:, :, 16:32], bb[64:128, ri, 4 * c:4 * c + 4, :]))
                        kb.op('pe', ['s_S4', 'identf'], ['pb1'],
                              lambda e: e.matmul(pb[1][:, 0:128], S4[:, ri, :], self.identf[:], start=True, stop=True))
                        kb.op('act', ['pb1'], ['s_BT'], lambda e: e.copy(BT[:, c, ri, :], pb[1][:, 0:128]))
                        dv(['s_cn', 's_msk'], ['s_M2'],
                           lambda e: e.tensor_scalar(M2[:, ri, 0:64], cn[:, ri, c, :], msk[:, 4:5], None, ALU.mult))
                        dv(['s_cn', 's_msk'], ['s_M2'],
                           lambda e: e.tensor_scalar(M2[:, ri, 64:128], cn[:, ri, c, :], msk[:, 5:6], None, ALU.mult))
                        kb.op('pe', ['s_M2', 'identf'], ['pb2'],
                              lambda e: e.matmul(pb[2][:, 0:128], M2[:, ri, :], self.identf[:], start=True, stop=True))
                        kb.op('act', ['pb2'], ['s_CT'],
                              lambda e: e.activation(CT[:, c, ri, :], pb[2][:, 0:128], AF.Copy, scale=(1.0 if ri == 0 else -1.0)))
                kb.barrier()
            with ExitStack() as SC:
                sbC = lambda n, sh, dt: kb.sb(n, sh, dt, SC)
                Xr = sbC("s_Xr", [128, S], F32)
                Xi = sbC("s_Xi", [128, S], F32)
                Xb = sbC("s_Xb", [128, 2, 1024], BF16)
                yc = sbC("s_yc", [128, S], F32)
                ub = sbC("s_ub", [128, S], BF16)
                BTm = sbC("s_BTm", [128, 2, 2, 128], BF16)
                Ez = sbC("s_Ez", [128, 2, 260], F32)
                tk = Xb[:].bitcast(F32)
                W1 = lambda k, q: pw[:, k, q:q + 1]
                KXr = [('sXr', j) for j in range(8)]
                KXi = [('sXi', j) for j in range(8)]
                XK = KXr + KXi
                xrv = Xr[:].rearrange("p (j c) -> p j c", j=8)
                xiv = Xi[:].rearrange("p (j c) -> p j c", j=8)
                Xb4 = Xb[:].rearrange("p r (j c) -> p r j c", j=8)
                dv([], ['sEr', 'sEi'], lambda e: e.memset(Ez[:], 0.0))
                wu = [self.load_w(w_in[:, 0:512], 512), self.load_w(w_in[:, 512:1024], 512)]
                for c in range(8):
                    whu, wku = wu[c // 4]
                    cc = (c % 4) * 128
                    for b in range(4):
                        bs = slice(b * 512, (b + 1) * 512)
                        pi = b % 2
                        xk = [('xT', b * 4 + j) for j in range(4)]
                        for kt in range(KT):
                            kb.op('pe', xk + [wku], ['pb%d' % pi],
                                  lambda e: e.matmul(pb[pi][:], whu[:, kt, cc:cc + 128], self.xT[:, kt, bs],
                                                     start=(kt == 0), stop=(kt == KT - 1)))
                        kb.op('act', ['pb%d' % pi], [('s_ub', b)], lambda e: e.copy(ub[:, bs], pb[pi][:]))
                        dv(['pb%d' % pi, 's_dd'], [('s_yc', b)],
                           lambda e: e.tensor_scalar(yc[:, bs], pb[pi][:], dd[:, c:c + 1], None, ALU.mult))
                    for r in range(4):
                        q = 4 * c + r
                        qs_ = q % 2
                        for ri in range(2):
                            kb.op('act', ['s_BT', 's_msk'], [('s_BTm', qs_)],
                                  lambda e: e.activation(BTm[:, qs_, ri, :], BT[:, c, ri, :], AF.Copy, scale=msk[:, r:r + 1]))
                        for b in range(4):
                            bs = slice(b * 512, (b + 1) * 512)
                            ui = b % 2
                            for ri, Xv, KX in ((0, xrv, KXr), (1, xiv, KXi)):
                                pi = 2 + 2 * ui + ri
                                kb.op('pe', [('s_ub', b), ('s_BTm', qs_)], ['pb%d' % pi],
                                      lambda e: e.matmul(pb[pi][:], BTm[:, qs_, ri, :],
                                                         ub[:, bs].rearrange("p (c j) -> p j c", j=8), start=True, stop=True))
                                if ri == 0:
                                    kb.op('act', ['pb%d' % pi], KX,
                                          lambda e: e.copy(Xv[:, :, b * 64:(b + 1) * 64], pb[pi][:].rearrange("p (j c) -> p j c", j=8)))
                                else:
                                    dv(['pb%d' % pi], KX,
                                       lambda e: e.tensor_copy(Xv[:, :, b * 64:(b + 1) * 64], pb[pi][:].rearrange("p (j c) -> p j c", j=8)))
                        for j in range(1, 8):
                            rj, rjm, ij, ijm = ('sXr', j), ('sXr', j - 1), ('sXi', j), ('sXi', j - 1)
                            dv([rjm, rj, 's_pw'], [rj], lambda e: e.scalar_tensor_tensor(xrv[:, j, :], xrv[:, j - 1, :], W1(0, q), xrv[:, j, :], ALU.mult, ALU.add))
                            dv([ijm, ij, 's_pw'], [ij], lambda e: e.scalar_tensor_tensor(xiv[:, j, :], xiv[:, j - 1, :], W1(0, q), xiv[:, j, :], ALU.mult, ALU.add))
                            dv([ijm, rj, 's_pw'], [rj], lambda e: e.scalar_tensor_tensor(xrv[:, j, :], xiv[:, j - 1, :], W1(2, q), xrv[:, j, :], ALU.mult, ALU.add))
                            dv([rjm, ij, 's_pw'], [ij], lambda e: e.scalar_tensor_tensor(xiv[:, j, :], xrv[:, j - 1, :], W1(1, q), xiv[:, j, :], ALU.mult, ALU.add))
                        dv([('sXr', 7)], ['sEr'], lambda e: e.tensor_copy(Ez[:, 0, 1:257], xrv[:, 7, :]))
                        dv([('sXi', 7)], ['sEi'], lambda e: e.tensor_copy(Ez[:, 1, 1:257], xiv[:, 7, :]))
                        TR, TI = 'sXbr', 'sXbi'
                        for k in range(8):
                            sft = 1 << k
                            n = 256 - sft
                            kq = 24 + 3 * k
                            lo = slice(1, 1 + n)
                            hi = slice(1 + sft, 257)
                            dv(['sEr', 's_pw'], [TR], lambda e: e.tensor_scalar(tk[:, 0, 0:n], Ez[:, 0, lo], W1(kq, q), None, ALU.mult))
                            dv(['sEi', 's_pw'], [TI], lambda e: e.tensor_scalar(tk[:, 1, 0:n], Ez[:, 1, lo], W1(kq, q), None, ALU.mult))
                            dv(['sEi', 's_pw', TR], [TR], lambda e: e.scalar_tensor_tensor(tk[:, 0, 0:n], Ez[:, 1, lo], W1(kq + 2, q), tk[:, 0, 0:n], ALU.mult, ALU.add))
                            dv(['sEr', 's_pw', TI], [TI], lambda e: e.scalar_tensor_tensor(tk[:, 1, 0:n], Ez[:, 0, lo], W1(kq + 1, q), tk[:, 1, 0:n], ALU.mult, ALU.add))
                            dv(['sEr', TR], ['sEr'], lambda e: e.tensor_tensor(Ez[:, 0, hi], Ez[:, 0, hi], tk[:, 0, 0:n], ALU.add))
                            dv(['sEi', TI], ['sEi'], lambda e: e.tensor_tensor(Ez[:, 1, hi], Ez[:, 1, hi], tk[:, 1, 0:n], ALU.add))
                        for hf in range(2):
                            cs = slice(hf * 128, (hf + 1) * 128)
                            xbr = Xb[:, 0, :].rearrange("p (c j) -> p c j", j=8)
                            xbi = Xb[:, 1, :].rearrange("p (c j) -> p c j", j=8)
                            for j in range(8):
                                pj = 3 * j
                                rj, ij = ('sXr', j), ('sXi', j)
                                dv([rj, 'sEr', 's_pw'], [rj], lambda e: e.scalar_tensor_tensor(xrv[:, j, cs], Ez[:, 0, cs], W1(pj, q), xrv[:, j, cs], ALU.mult, ALU.add))
                                dv([ij, 'sEi', 's_pw'], [ij], lambda e: e.scalar_tensor_tensor(xiv[:, j, cs], Ez[:, 1, cs], W1(pj, q), xiv[:, j, cs], ALU.mult, ALU.add))
                                dv([rj, 'sEi', 's_pw'], ['sXbr'], lambda e: e.scalar_tensor_tensor(Xb4[:, 0, j, :], Ez[:, 1, cs], W1(pj + 2, q), xrv[:, j, cs], ALU.mult, ALU.add))
                                dv([ij, 'sEr', 's_pw'], ['sXbi'], lambda e: e.scalar_tensor_tensor(Xb4[:, 1, j, :], Ez[:, 0, cs], W1(pj + 1, q), xiv[:, j, cs], ALU.mult, ALU.add))
                            for bb_ in range(2):
                                b = 2 * hf + bb_
                                bs = slice(b * 512, (b + 1) * 512)
                                ls = slice(bb_ * 512, (bb_ + 1) * 512)
                                pi = 6 + bb_
                                for ri, xk_ in ((0, 'sXbr'), (1, 'sXbi')):
                                    kb.op('pe', [xk_, 's_CT'], ['pb%d' % pi],
                                          lambda e: e.matmul(pb[pi][:], CT[:, c, ri, :], Xb4[:, ri, :, bb_ * 64:(bb_ + 1) * 64],
                                                             start=(ri == 0), stop=(ri == 1)))
                                ycv = yc[:, bs].rearrange("p (c j) -> p c j", j=8)
                                dv(['pb%d' % pi, 's_msk', ('s_yc', b)], [('s_yc', b)],
                                   lambda e: e.scalar_tensor_tensor(ycv, pb[pi][:].rearrange("p (j c) -> p c j", j=8),
                                                                    msk[:, r:r + 1], ycv, ALU.mult, ALU.add))
                    YK = [('s_yc', b) for b in range(4)]
                    kb.op('act', YK, XK, lambda e: e.activation(Xr[:], yc[:], AF.Square))
                    dv(XK, XK, lambda e: e.tensor_scalar(Xr[:], Xr[:], 0.044715, 1.0, ALU.mult, ALU.add))
                    dv(XK + YK, XK, lambda e: e.tensor_tensor(Xr[:], Xr[:], yc[:], ALU.mult))
                    kb.op('act', XK, XK, lambda e: e.activation(Xi[:], Xr[:], AF.Sigmoid, scale=1.5957691216057308))
                    dv(XK + YK, [('oT', tt) for tt in range(NT)], lambda e: e.tensor_tensor(self.oT[:, c, :], yc[:], Xi[:], ALU.mult))
                kb.barrier()
            with ExitStack() as SG:
                sbG = lambda n, sh, dt: kb.sb(n, sh, dt, SG)
                wg = sbG("s_wg", [128, KT, D], BF16)
                to = sbG("s_to", [128, 8, 512], BF16)
                sgl = sbG("s_sgl", [128, 512], F32)
                sgt = sbG("s_sgt", [128, 512], F32)
                for r4 in range(4):
                    kb.dma('sp', self.wst1[:, :, :], A[p + "w_glu"][:, r4 * 256:(r4 + 1) * 256].rearrange("(kt p) c -> p kt c", p=128),
                           [], ['wst0'], 'wst0')
                    kb.op('pool', ['wst0'], ['s_wg'], lambda e: e.tensor_copy(wg[:, :, r4 * 256:(r4 + 1) * 256], self.wst1[:, :, :]))
                wgt = [self.load_w(w_in[:, 1024:1536], 512), self.load_w(w_in[:, 1536:2048], 512)]
                for b in range(4):
                    bs = slice(b * 512, (b + 1) * 512)
                    ok = [('oT', b * 4 + j) for j in range(4)]
                    xk = [('xT', b * 4 + j) for j in range(4)]
                    for c in range(8):
                        whg, wkg = wgt[c // 4]
                        cc = (c % 4) * 128
                        for kt in range(KT):
                            kb.op('pe', ok + ['s_wg'], ['pb0'],
                                  lambda e: e.matmul(pb[0][:], wg[:, kt, c * 128:(c + 1) * 128], self.oT[:, kt, bs],
                                                     start=(kt == 0), stop=(kt == KT - 1)))
                        for kt in range(KT):
                            kb.op('pe', xk + [wkg], ['pb1'],
                                  lambda e: e.matmul(pb[1][:], whg[:, kt, cc:cc + 128], self.xT[:, kt, bs],
                                                     start=(kt == 0), stop=(kt == KT - 1)))
                        kb.op('act', ['pb0', 's_bg'], ['s_sgl'],
                              lambda e: e.activation(sgl[:], pb[0][:], AF.Sigmoid, bias=bg[:, c:c + 1], scale=1.0))
                        kb.op('act', ['pb1'], ['s_sgt'], lambda e: e.activation(sgt[:], pb[1][:], AF.Silu))
                        dv(['s_sgl', 's_sgt'], ['s_sgl'], lambda e: e.tensor_tensor(sgl[:], sgl[:], sgt[:], ALU.mult))
                        dv(ok + ['s_sgl'], ['s_to'], lambda e: e.tensor_tensor(to[:, c, :], self.oT[:, c, bs], sgl[:], ALU.mult))
                    kb.op('pool', ['s_to'], ok, lambda e: e.tensor_copy(self.oT[:, :, bs], to[:]))
                kb.barrier()


_CACHE = {}


def kernel(**inputs):
    nseq = 16 // N_CORES
    if 'nc' not in _CACHE:
        _CACHE['nc'] = build_program(nseq)
    nc = _CACHE['nc']
    x = np.ascontiguousarray(inputs["x"], dtype=np.float32)
    in_maps = []
    for c in range(N_CORES):
        m = {k: np.ascontiguousarray(v, dtype=np.float32) for k, v in inputs.items() if k != "x"}
        m["x"] = x[c * nseq:(c + 1) * nseq]
        in_maps.append(m)
    res = run_bass_kernel_spmd(nc, in_maps, core_ids=list(range(N_CORES)))
    return np.concatenate([r["out"] for r in res.results], axis=0)
```

```python
import math
import os
CUT = int(os.environ.get('KCUT', '0'))
import numpy as np
from contextlib import ExitStack
import concourse.bass as bass
import concourse.mybir as mybir
from concourse.bass_utils import run_bass_kernel_spmd

F32 = mybir.dt.float32
BF16 = mybir.dt.bfloat16
I32 = mybir.dt.int32
AF = mybir.ActivationFunctionType
ALU = mybir.AluOpType
AX = mybir.AxisListType

D = 1024
S = 2048
NH = 8
HD = 128
KT = 8
NT = S // 128
DEPTH = 4
ALPHA = (2 * DEPTH) ** 0.25
LN_EPS = 1e-5
RMS_EPS = 1e-6
CH = 64
N_CORES = 8
NO_SELF_SYNC = ('pe',)


class KB:
    def __init__(self, nc, es):
        self.nc, self.es = nc, es
        self.eng = {'pe': nc.tensor, 'dve': nc.vector, 'act': nc.scalar, 'pool': nc.gpsimd, 'sp': nc.sync}
        self.semh = {}
        self.cnt = {}
        for e in ('pe', 'dve', 'act', 'pool'):
            self.semh['s_' + e] = es.enter_context(nc.semaphore('s_' + e))
            self.cnt[e] = 0
        self.waited = {e: {} for e in self.eng}
        self.bufs = {}
        self.dsem = {}
        self.nins = 0

    def sb(self, name, shape, dt, es=None):
        self.nalloc = getattr(self, 'nalloc', 0) + 1
        return (es or self.es).enter_context(self.nc.sbuf_tensor("%s_%d" % (name, self.nalloc), shape, dt))

    def barrier(self):
        evs = [('s_' + e, self.cnt[e]) for e in ('pe', 'dve', 'act', 'pool')]
        evs += [(d[0], d[1]) for d in self.dsem.values()]
        for eng in ('pe', 'dve', 'act', 'pool', 'sp'):
            for n, v in evs:
                if v == 0 or self.waited[eng].get(n, 0) >= v:
                    continue
                if n == 's_' + eng and eng in NO_SELF_SYNC:
                    continue
                self.eng[eng].wait_ge(self.semh[n], v)
                self.waited[eng][n] = v

    def ps(self, name, shape, dt):
        return self.es.enter_context(self.nc.psum_tensor(name, shape, dt))

    def _deps(self, eng, reads, writes):
        need = {}

        def add(ev):
            if ev is not None and need.get(ev[0], 0) < ev[1]:
                need[ev[0]] = ev[1]
        for k in reads:
            b = self.bufs.get(k)
            if b:
                add(b[0])
        for k in writes:
            b = self.bufs.get(k)
            if b:
                add(b[0])
                for n, v in b[1].items():
                    add((n, v))
        own = 's_' + eng
        for n, v in need.items():
            if n == own and eng in NO_SELF_SYNC:
                continue
            if self.waited[eng].get(n, 0) >= v:
                continue
            self.eng[eng].wait_ge(self.semh[n], v)
            self.waited[eng][n] = v

    def _record(self, ev, reads, writes):
        for k in reads:
            b = self.bufs.setdefault(k, [None, {}])
            if b[1].get(ev[0], 0) < ev[1]:
                b[1][ev[0]] = ev[1]
        for k in writes:
            self.bufs[k] = [ev, {}]

    def op(self, eng, reads, writes, fn):
        px = [k for k in reads if isinstance(k, str) and k.startswith('pb') and eng != 'pe']
        if px:
            writes = list(writes) + px
        self._deps(eng, reads, writes)
        ins = fn(self.eng[eng])
        self.cnt[eng] += 1
        ins.then_inc(self.semh['s_' + eng], 1)
        self._record(('s_' + eng, self.cnt[eng]), reads, writes)
        self.nins += 1

    def dma(self, q, out, in_, reads, writes, key, **kw):
        self._deps(q, reads, writes)
        if key not in self.dsem:
            nm = 'd%d' % len(self.dsem)
            self.semh[nm] = self.es.enter_context(self.nc.semaphore(nm))
            self.dsem[key] = [nm, 0]
        d = self.dsem[key]
        d[1] += 16
        self.eng[q].dma_start(out=out, in_=in_, **kw).then_inc(self.semh[d[0]], 16)
        self._record((d[0], d[1]), reads, writes)
        self.nins += 1

    def final_wait(self, q, keys):
        for k in keys:
            d = self.dsem[k]
            self.eng[q].wait_ge(self.semh[d[0]], d[1])


LAYER_KINDS = [0, 1, 2, 0]


def declare_inputs(nc, nseq):
    a = {}

    def inp(name, shape):
        a[name] = nc.dram_tensor(name, list(shape), F32, kind="ExternalInput").ap()
    inp("x", (nseq, S, D))
    inp("hgrn_lower_bounds", (DEPTH, D))
    for i, kind in enumerate(LAYER_KINDS):
        p = "l%d_" % i
        if kind == 0:
            inp(p + "w_in", (D, 4 * D)); inp(p + "norm_g", (D,)); inp(p + "w_out", (D, D))
        elif kind == 1:
            inp(p + "w_in", (D, 4 * D)); inp(p + "w_out", (D, D))
        else:
            inp(p + "w_in", (D, 2 * D))
            inp(p + "a_re", (64, 64)); inp(p + "a_im", (64, 64)); inp(p + "log_dt", (64,))
            inp(p + "b_re", (64, 64, 16)); inp(p + "b_im", (64, 64, 16))
            inp(p + "c_re", (64, 16, 64)); inp(p + "c_im", (64, 16, 64))
            inp(p + "d", (D,)); inp(p + "w_glu", (D, D)); inp(p + "b_glu", (D,)); inp(p + "w_out", (D, D))
        inp(p + "ln_g", (D,)); inp(p + "ln_b", (D,))
    return a


def build_program(nseq, layers=(0, 1, 2, 3), dbg=False):
    nc = bass.Bass("TRN2", target_bir_lowering=False)
    A = declare_inputs(nc, nseq)
    out = nc.dram_tensor("out", [nseq, S, D], F32, kind="ExternalOutput").ap()
    emit(nc, A, out, nseq, layers)
    return nc


def emit(nc, A, out, nseq, layers):
    with ExitStack() as es:
        kb = KB(nc, es)
        P = Prog(kb, A, out, nseq, layers)
        P.run()


class Prog:
    def __init__(self, kb, A, out, nseq, layers):
        self.kb, self.A, self.out, self.nseq, self.layers = kb, A, out, nseq, layers
        self.nc = kb.nc
        kb_ = kb
        self.xres = kb_.sb("xres", [128, NT, D], F32)
        self.xT = kb_.sb("xT", [128, KT, S], BF16)
        self.oT = kb_.sb("oT", [128, KT, S], BF16)
        self.wst1 = kb_.sb("wst0", [128, KT, 256], F32)
        self.wst = [self.wst1, self.wst1]
        self.wh = [kb_.sb("wh%d" % i, [128, KT, 512], BF16) for i in range(2)]
        self.ident = kb_.sb("ident", [128, 128], BF16)
        self.identf = kb_.sb("identf", [128, 128], F32)
        self.zb = kb_.sb("zb", [128, D], BF16)
        self.junk = self.zb
        self.st = kb_.sb("stats", [128, 16], F32)
        self.pb = [kb_.ps("pb%d" % i, [128, 512], F32) for i in range(8)]
        self.wslot = 0

    def consts(self):
        kb = self.kb
        it = kb.sb("iota_t", [128, 128], F32)
        kb.op('pool', [], ['iota_t'], lambda e: e.iota(it[:], [[1, 128]], base=0, channel_multiplier=-1,
                                                       allow_small_or_imprecise_dtypes=True))
        kb.op('dve', ['iota_t'], ['identf'], lambda e: e.tensor_single_scalar(self.identf[:], it[:], 0.0, ALU.is_equal))
        kb.op('dve', ['identf'], ['ident'], lambda e: e.tensor_copy(self.ident[:], self.identf[:]))
        self.iota_t = it

    def load_w(self, src_ap, ncols, key=None):
        kb = self.kb
        i = self.wslot
        self.wslot ^= 1
        st, wh = self.wst1, self.wh[i]
        for r in range(2):
            kb.dma('sp', st[:, :, :], src_ap[:, r * 256:(r + 1) * 256].rearrange("(kt p) c -> p kt c", p=128),
                   [], ['wst0'], 'wst0')
            kb.op('pool', ['wst0'], ['wh%d' % i], lambda e: e.tensor_copy(wh[:, :, r * 256:(r + 1) * 256], st[:, :, :]))
        return wh, 'wh%d' % i

    def load_head_w(self, w_in, h, nstream):
        kb = self.kb
        i = self.wslot
        self.wslot ^= 1
        st, wh = self.wst1, self.wh[i]
        for r in range(nstream // 2):
            for jj in range(2):
                j = 2 * r + jj
                kb.dma('sp', st[:, :, jj * 128:(jj + 1) * 128],
                       w_in[:, j * D + h * 128: j * D + (h + 1) * 128].rearrange("(kt p) c -> p kt c", p=128),
                       [], ['wst0'], 'wst0')
            kb.op('pool', ['wst0'], ['wh%d' % i], lambda e: e.tensor_copy(wh[:, :, r * 256:(r + 1) * 256], st[:, :, :]))
        return wh, 'wh%d' % i

    def load_x(self, s):
        kb = self.kb
        for tt in range(NT):
            kb.dma('sp', self.xres[:, tt, :], self.A["x"][s, tt * 128:(tt + 1) * 128, :], [], [('xres', tt)], ('xres', tt))

    def bf(self, i):
        return self.pb[i][:].bitcast(BF16)

    def make_xT(self, tt, src_bf):
        kb = self.kb
        for g in range(2):
            pi = 7 - g
            pt = self.pb[pi]
            for k4 in range(4):
                kt = g * 4 + k4
                kb.op('pe', ['zb', 'ident'], ['pb%d' % pi],
                      lambda e: e.matmul(pt[:, k4 * 128:(k4 + 1) * 128], src_bf[:, kt * 128:(kt + 1) * 128], self.ident[:],
                                         start=True, stop=True))
            kb.op('act', ['pb%d' % pi], [('xT', tt)],
                  lambda e: e.copy(self.xT[:, g * 4:(g + 1) * 4, tt * 128:(tt + 1) * 128],
                                   pt[:, :].rearrange("p (k t) -> p k t", k=4)))

    def init_xT(self):
        kb = self.kb
        for tt in range(NT):
            kb.op('dve', [('xres', tt)], ['zb'], lambda e: e.tensor_copy(self.zb[:], self.xres[:, tt, :]))
            self.make_xT(tt, self.zb)

    def outproj_ln(self, li):
        kb = self.kb
        A = self.A
        p = "l%d_" % li
        self.lng = kb.sb("lng", [128, D], F32, self.les)
        self.lnb = kb.sb("lnb", [128, D], F32, self.les)
        kb.dma('sp', self.lng[:], A[p + "ln_g"].partition_broadcast(128), [], ['lng'], 'lng')
        kb.dma('sp', self.lnb[:], A[p + "ln_b"].partition_broadcast(128), [], ['lnb'], 'lnb')
        st = self.st
        for half in range(2):
            hs = slice(half * 512, (half + 1) * 512)
            wh, wk = self.load_w(A[p + "w_out"][:, hs], 512, None)
            for tt in range(NT):
                ts = slice(tt * 128, (tt + 1) * 128)
                pi = 5 + (tt % 2)
                pbk = self.pb[pi]
                for kt in range(KT):
                    kb.op('pe', [('oT', tt), wk], ['pb%d' % pi],
                          lambda e: e.matmul(pbk[:], self.oT[:, kt, ts], wh[:, kt, :],
                                             start=(kt == 0), stop=(kt == KT - 1)))
                kb.op('dve', ['pb%d' % pi, ('xres', tt)], [('xres', tt)],
                      lambda e: e.scalar_tensor_tensor(self.xres[:, tt, hs], self.xres[:, tt, hs], ALPHA, pbk[:],
                                                       ALU.mult, ALU.add))
        for tt in range(NT):
            z = self.xres[:, tt, :]
            kb.op('act', [('xres', tt)], ['zb', 'st_ss'],
                  lambda e: e.activation(self.junk[:], z, AF.Square, accum_out=st[:, 0:1]))
            kb.op('dve', [('xres', tt)], ['st_sum'], lambda e: e.tensor_reduce(st[:, 1:2], z, AX.X, ALU.add))
            kb.op('dve', ['st_sum'], ['st_mean'], lambda e: e.tensor_scalar(st[:, 2:3], st[:, 1:2], 1.0 / D, None, ALU.mult))
            kb.op('dve', ['st_mean'], ['st_m2'], lambda e: e.tensor_tensor(st[:, 3:4], st[:, 2:3], st[:, 2:3], ALU.mult))
            kb.op('dve', ['st_ss', 'st_m2'], ['st_var'],
                  lambda e: e.scalar_tensor_tensor(st[:, 4:5], st[:, 0:1], 1.0 / D, st[:, 3:4], ALU.mult, ALU.subtract))
            kb.op('dve', ['st_var'], ['st_var'],
                  lambda e: e.tensor_scalar(st[:, 4:5], st[:, 4:5], LN_EPS, None, ALU.add))
            kb.op('act', ['st_var'], ['st_sd'], lambda e: e.activation(st[:, 6:7], st[:, 4:5], AF.Sqrt))
            kb.op('dve', ['st_sd'], ['st_rstd'], lambda e: e.reciprocal(st[:, 5:6], st[:, 6:7]))
            kb.op('dve', [('xres', tt), 'st_mean', 'st_rstd'], [('xres', tt)],
                  lambda e: e.tensor_scalar(z, z, st[:, 2:3], st[:, 5:6], ALU.subtract, ALU.mult))
            kb.op('dve', [('xres', tt), 'lng'], [('xres', tt)], lambda e: e.tensor_tensor(z, z, self.lng[:], ALU.mult))
            kb.op('dve', [('xres', tt), 'lnb'], [('xres', tt)], lambda e: e.tensor_tensor(z, z, self.lnb[:], ALU.add))
            kb.op('act', [('xres', tt)], ['zb'], lambda e: e.copy(self.zb[:], z))
            self.make_xT(tt, self.zb)

    def store_out(self, s):
        kb = self.kb
        for tt in range(NT):
            kb.dma('sp', self.out[s, tt * 128:(tt + 1) * 128, :], self.xres[:, tt, :], [('xres', tt)], [], ('st', tt))

    def run(self):
        kb = self.kb
        self.consts()
        self.setup_layers()
        for s in range(self.nseq):
            self.load_x(s)
            self.init_xT()
            for li in self.layers:
                kind = LAYER_KINDS[li]
                with ExitStack() as les:
                    self.les = les
                    if kind == 0:
                        self.hgrn(li)
                    elif kind == 1:
                        self.moba(li)
                    else:
                        self.s5(li)
                    kb.barrier()
                if CUT == 0 or CUT == 6:
                    with ExitStack() as les:
                        self.les = les
                        self.outproj_ln(li)
                        kb.barrier()
            self.store_out(s)
        kb.final_wait('sp', [('st', tt) for tt in range(NT)])

    def setup_layers(self):
        self.setup_hgrn()
        self.setup_moba()

    def setup_hgrn(self):
        kb = self.kb
        A = self.A
        lbr = kb.sb("lbraw", [128, DEPTH, NH], F32)
        self.lb = kb.sb("lb", [128, DEPTH, NH], F32)
        self.oml = kb.sb("oml", [128, DEPTH, NH], F32)
        sm = kb.sb("lbsum", [128, NH], F32)
        for l in range(DEPTH):
            kb.dma('sp', lbr[:, l, :], A["hgrn_lower_bounds"][l, :].rearrange("(h p) -> p h", p=128), [], ['lbraw'], 'lbraw',
                   allow_slow_non_contiguous=True)
        kb.op('act', ['lbraw'], ['lbraw'], lambda e: e.activation(lbr[:], lbr[:], AF.Exp))
        kb.op('dve', ['lbraw'], ['lbsum'], lambda e: e.tensor_tensor(sm[:], lbr[:, 0, :], lbr[:, 1, :], ALU.add))
        kb.op('dve', ['lbraw', 'lbsum'], ['lbsum'], lambda e: e.tensor_tensor(sm[:], sm[:], lbr[:, 2, :], ALU.add))
        kb.op('dve', ['lbraw', 'lbsum'], ['lbsum'], lambda e: e.tensor_tensor(sm[:], sm[:], lbr[:, 3, :], ALU.add))
        kb.op('dve', ['lbsum'], ['lbsum'], lambda e: e.reciprocal(sm[:], sm[:]))
        for l in range(DEPTH):
            kb.op('dve', ['lbraw', 'lbsum'], ['lbraw'],
                  lambda e: e.tensor_tensor(lbr[:, l, :], lbr[:, l, :], sm[:], ALU.mult))
        kb.op('dve', [], ['lb'], lambda e: e.memset(self.lb[:], 0.0))
        for i in range(1, DEPTH):
            kb.op('dve', ['lb', 'lbraw'], ['lb'],
                  lambda e: e.tensor_tensor(self.lb[:, i, :], self.lb[:, i - 1, :], lbr[:, i, :], ALU.add))
        kb.op('dve', ['lb'], ['oml'], lambda e: e.tensor_scalar(self.oml[:], self.lb[:], -1.0, 1.0, ALU.mult, ALU.add))
        self.amask = kb.sb("amask", [128, 128], F32)
        tmp = kb.sb("amtmp", [128, 128], F32)
        it = self.iota_t
        kb.op('dve', ['iota_t'], ['amask'], lambda e: e.tensor_single_scalar(self.amask[:], it[:], 0.0, ALU.is_ge))
        kb.op('dve', ['amask'], ['amask'], lambda e: e.memset(self.amask[0:64, 64:128], 0.0))

    def alloc_hgrn(self):
        kb = self.kb
        _sb = kb.sb
        kb_sb = lambda n, sh, dt: _sb(n, sh, dt, self.les)
        class _K:
            sb = staticmethod(kb_sb)
        kb = _K
        self.hq = kb.sb("hq", [128, 512], F32)
        self.hf = kb.sb("hf", [128, 512], F32)
        self.hlog = kb.sb("hlog", [128, 512], F32)
        self.hkg = kb.sb("hkg", [128, 512], F32)
        self.hbc = kb.sb("hbc", [128, 512], F32)
        self.heb = kb.sb("heb", [128, 512], F32)
        self.henb = kb.sb("henb", [128, 512], F32)
        self.hqt = kb.sb("hqt", [128, 512], BF16)
        self.hkt = kb.sb("hkt", [128, 512], BF16)
        self.hv = kb.sb("hv", [128, 4, 128], BF16)
        self.hsg = kb.sb("hsg", [128, 4, 128], F32)
        self.hktok = kb.sb("hktok", [128, 4, 128], BF16)
        self.ham = kb.sb("ham", [128, 4, 128], BF16)
        self.hSall = kb.sb("hSall", [128, 9, 128], F32)
        self.hKVs = kb.sb("hKVs", [128, 8, 128], F32)
        self.hSb = kb.sb("hSb", [128, 8, 128], BF16)
        self.hon = kb.sb("hon", [128, 4, 128], BF16)
        self.hng = kb.sb("hng", [128, NH], F32)
        self.hst = kb.sb("hst", [128, 16], F32)

    def hgrn(self, li):
        self.alloc_hgrn()
        kb = self.kb
        self.rmask = kb.sb("rmask", [128, 512], F32, self.les)
        kb.op('dve', [], ['rmask'], lambda e: e.memset(self.rmask[:], 1.0))
        kb.op('dve', ['rmask'], ['rmask'],
              lambda e: e.memset(self.rmask[:].rearrange("p (c j) -> p c j", j=CH)[:, :, 0:1], 0.0))
        A = self.A
        p = "l%d_" % li
        w_in = A[p + "w_in"]
        kb.dma('sp', self.hng[:], A[p + "norm_g"].rearrange("(h p) -> p h", p=128), [], ['hng'], 'hng',
               allow_slow_non_contiguous=True)
        pb = self.pb
        for h in range(NH):
            wh, wk = self.load_head_w(w_in, h, 4)
            if CUT == 1:
                return
            lbc = self.lb[:, li, h:h + 1]
            omc = self.oml[:, li, h:h + 1]
            if CUT == 5 and h == 1:
                return
            kb.op('dve', [], [('hSall', 0)], lambda e: e.memset(self.hSall[:, 0, :], 0.0))
            for blk in range(S // 512):
                t0 = blk * 512
                tts = [blk * 4 + j for j in range(4)]
                xk = [('xT', tt) for tt in tts]
                for kt in range(KT):
                    kb.op('pe', xk + [wk], ['pb0'],
                          lambda e: e.matmul(pb[0][:], wh[:, kt, 0:128], self.xT[:, kt, t0:t0 + 512],
                                             start=(kt == 0), stop=(kt == KT - 1)))
                for kt in range(KT):
                    kb.op('pe', xk + [wk], ['pb1'],
                          lambda e: e.matmul(pb[1][:], wh[:, kt, 128:256], self.xT[:, kt, t0:t0 + 512],
                                             start=(kt == 0), stop=(kt == KT - 1)))
                kb.op('act', ['pb0'], ['hq'], lambda e: e.activation(self.hq[:], pb[0][:], AF.Silu))
                kb.op('act', ['pb1'], ['hf'], lambda e: e.activation(self.hf[:], pb[1][:], AF.Sigmoid))
                kb.op('dve', ['hf', 'lb', 'oml'], ['hf'],
                      lambda e: e.tensor_scalar(self.hf[:], self.hf[:], omc, lbc, ALU.mult, ALU.add))
                kb.op('act', ['hf'], ['hlog'], lambda e: e.activation(self.hlog[:], self.hf[:], AF.Ln))
                kb.op('dve', ['hf'], ['hkg'],
                      lambda e: e.tensor_scalar(self.hkg[:], self.hf[:], -1.0, 1.0, ALU.mult, ALU.add))
                kb.op('dve', ['hlog', 'rmask'], ['hbc'],
                      lambda e: e.tensor_tensor_scan(self.hbc[:], self.rmask[:], self.hlog[:], 0.0, ALU.mult, ALU.add))
                kb.op('act', ['hbc'], ['heb'], lambda e: e.activation(self.heb[:], self.hbc[:], AF.Exp))
                kb.op('act', ['hbc'], ['henb'], lambda e: e.activation(self.henb[:], self.hbc[:], AF.Exp, scale=-1.0))
                kb.op('dve', ['hq', 'heb'], ['hqt'], lambda e: e.tensor_tensor(self.hqt[:], self.hq[:], self.heb[:], ALU.mult))
                kb.op('dve', ['hkg', 'henb'], ['hkt'], lambda e: e.tensor_tensor(self.hkt[:], self.hkg[:], self.henb[:], ALU.mult))
                if CUT == 2:
                    return
                for j in range(4):
                    tt = tts[j]
                    ts = slice(tt * 128, (tt + 1) * 128)
                    ls = slice(j * 128, (j + 1) * 128)
                    for kt in range(KT):
                        kb.op('pe', [('xT', tt), wk], ['pb2'],
                              lambda e: e.matmul(pb[2][:, 0:256], self.xT[:, kt, ts], wh[:, kt, 256:512],
                                                 start=(kt == 0), stop=(kt == KT - 1)))
                    kb.op('dve', ['pb2'], [('hv', j)], lambda e: e.tensor_copy(self.hv[:, j, :], pb[2][:, 0:128]))
                    kb.op('act', ['pb2'], [('hsg', j)], lambda e: e.activation(self.hsg[:, j, :], pb[2][:, 128:256], AF.Silu))
                    kb.op('pe', ['hkt', 'hqt'], ['pb3'],
                          lambda e: e.matmul(pb[3][:, ls], self.hkt[:, ls], self.hqt[:, ls], start=True, stop=True))
                    kb.op('pe', ['hkt', 'ident'], ['pb6'],
                          lambda e: e.matmul(pb[6][:, ls], self.hkt[:, ls], self.ident[:], start=True, stop=True))
                kb.op('dve', ['pb3', 'amask'], ['ham'],
                      lambda e: e.tensor_tensor(self.ham[:], pb[3][:].rearrange("p (j t) -> p j t", j=4),
                                                self.amask[:].unsqueeze(1).broadcast_to([128, 4, 128]), ALU.mult))
                kb.op('act', ['pb6'], ['hktok'],
                      lambda e: e.copy(self.hktok[:], pb[6][:].rearrange("p (j t) -> p j t", j=4)))
                if CUT == 3:
                    return
                for c in range(8):
                    j, half = c // 2, c % 2
                    rs = slice(half * 64, half * 64 + 64)
                    bank = 4 if half == 0 else 7
                    reg = pb[bank][:, j * 128:(j + 1) * 128]
                    kb.op('pe', ['hktok', ('hv', j)], ['pb%d' % bank],
                          lambda e: e.matmul(reg, self.hktok[rs, j, :], self.hv[rs, j, :], start=True, stop=True))
                if CUT == 41:
                    return
                for c in range(8):
                    bank = 4 if c % 2 == 0 else 7
                    reg = pb[bank][:, (c // 2) * 128:(c // 2 + 1) * 128]
                    ebc = self.heb[:, c * 64 + 63: c * 64 + 64]
                    kb.op('act', ['pb%d' % bank, 'heb'], [('hKVs', c)],
                          lambda e: e.activation(self.hKVs[:, c, :], reg, AF.Copy, scale=ebc))
                if CUT == 42:
                    return
                for c in range(8):
                    ebc = self.heb[:, c * 64 + 63: c * 64 + 64]
                    kb.op('dve', [('hSall', c), ('hKVs', c), 'heb'], [('hSall', c + 1)],
                          lambda e: e.scalar_tensor_tensor(self.hSall[:, c + 1, :], self.hSall[:, c, :], ebc,
                                                           self.hKVs[:, c, :], ALU.mult, ALU.add))
                if CUT == 43:
                    return
                kb.op('act', [('hSall', c) for c in range(8)], ['hSb'],
                      lambda e: e.copy(self.hSb[:], self.hSall[:, 0:8, :]))
                kb.op('dve', [('hSall', 8)], [('hSall', 0)],
                      lambda e: e.tensor_copy(self.hSall[:, 0, :], self.hSall[:, 8, :]))
                if CUT == 4:
                    return
                hst = self.hst
                for j in range(4):
                    ls = slice(j * 128, (j + 1) * 128)
                    kb.op('pe', ['ham', ('hv', j)], ['pb5'],
                          lambda e: e.matmul(pb[5][:, ls], self.ham[:, j, :], self.hv[:, j, :], start=True, stop=False))
                    for half in range(2):
                        c = 2 * j + half
                        cs = slice(j * 128 + half * 64, j * 128 + half * 64 + 64)
                        kb.op('pe', ['hqt', 'hSb'], ['pb5'],
                              lambda e: e.matmul(pb[5][half * 64:half * 64 + 64, ls], self.hqt[:, cs], self.hSb[:, c, :],
                                                 start=False, stop=True))
                for j in range(4):
                    ls = slice(j * 128, (j + 1) * 128)
                    kb.op('act', ['pb5'], ['zb', 'hst0'],
                          lambda e: e.activation(self.junk[:, ls], pb[5][:, ls], AF.Square, accum_out=hst[:, j:j + 1]))
                kb.op('dve', ['hst0'], ['hst1'],
                      lambda e: e.tensor_scalar(hst[:, 4:8], hst[:, 0:4], 1.0 / HD, RMS_EPS, ALU.mult, ALU.add))
                kb.op('act', ['hst1'], ['hst1b'], lambda e: e.activation(hst[:, 8:12], hst[:, 4:8], AF.Sqrt))
                kb.op('dve', ['hst1b'], ['hst2'], lambda e: e.reciprocal(hst[:, 12:16], hst[:, 8:12]))
                for j in range(4):
                    ls = slice(j * 128, (j + 1) * 128)
                    kb.op('dve', ['pb5', 'hst2', ('hsg', j)], [('hon', j)],
                          lambda e: e.scalar_tensor_tensor(self.hon[:, j, :], pb[5][:, ls], hst[:, 12 + j:13 + j],
                                                           self.hsg[:, j, :], ALU.mult, ALU.mult))
                for j in range(4):
                    ls = slice(j * 128, (j + 1) * 128)
                    kb.op('pe', [('hon', j), 'ident'], ['pb6'],
                          lambda e: e.matmul(pb[6][:, ls], self.hon[:, j, :], self.ident[:], start=True, stop=True))
                kb.op('act', ['pb6', 'hng'], [('oT', tt) for tt in tts],
                      lambda e: e.activation(self.oT[:, h, t0:t0 + 512], pb[6][:], AF.Copy, scale=self.hng[:, h:h + 1]))

    def setup_moba(self):
        kb = self.kb
        it = self.iota_t
        self.pswap = kb.sb("pswap", [128, 128], F32)
        kb.op('dve', ['iota_t'], ['pswap'], lambda e: e.tensor_single_scalar(self.pswap[:], it[:], 64.0, ALU.is_equal))
        kb.op('dve', ['iota_t', 'pswap'], ['pswap'],
              lambda e: e.scalar_tensor_tensor(self.pswap[:], it[:], -64.0, self.pswap[:], ALU.is_equal, ALU.add))

    def moba(self, li):
        kb = self.kb
        A = self.A
        les = self.les
        sb = lambda n, sh, dt: kb.sb(n, sh, dt, les)
        pb = self.pb
        p = "l%d_" % li
        w_in = A[p + "w_in"]
        cosT = sb("m_cos", [128, S], F32)
        sinT = sb("m_sin", [128, S], F32)
        mq = sb("m_q", [128, 512], F32)
        mk = sb("m_k", [128, 512], F32)
        t1 = sb("m_t1", [128, 512], F32)
        qf = mq
        qT = sb("m_qT", [128, S], BF16)
        kT = sb("m_kT", [128, S], BF16)
        v1 = sb("m_v1", [128, NT, 132], BF16)
        sg = sb("m_sg", [128, NT, 128], BF16)
        sel = sb("m_sel", [128, NT, 8], F32)
        km = sb("m_km", [128, 8], F32)
        g8 = sb("m_g8", [128, 8], F32)
        mx8 = sb("m_mx8", [128, 8], F32)
        pT = [sb("m_pT%d" % i, [128, 256], BF16) for i in range(2)]
        cm = sb("m_cm", [128, 2, 256], BF16)
        acc = sb("m_acc", [128, 2, 132], F32)
        mon = sb("m_on", [128, 128], BF16)
        sc = sb("m_sc", [128, 8], F32)
        kb.op('pool', [], ['m_sc'], lambda e: e.iota(sc[:, 0:1], [[0, 1]], base=0, channel_multiplier=1,
                                                     allow_small_or_imprecise_dtypes=True))
        kb.op('dve', ['m_sc'], ['m_sc'], lambda e: e.tensor_scalar(sc[:, 5:6], sc[:, 0:1], 64.0, -64.0, ALU.is_ge, ALU.mult))
        kb.op('dve', ['m_sc'], ['m_sc'], lambda e: e.tensor_tensor(sc[:, 1:2], sc[:, 0:1], sc[:, 5:6], ALU.add))
        kb.op('act', ['m_sc'], ['m_sc'],
              lambda e: e.activation(sc[:, 2:3], sc[:, 1:2], AF.Exp, scale=-math.log(10000.0) / 64.0))
        kb.op('dve', ['m_sc'], ['m_sc'],
              lambda e: e.tensor_scalar(sc[:, 3:4], sc[:, 0:1], 64.0, 2.0, ALU.is_ge, ALU.mult))
        kb.op('dve', ['m_sc'], ['m_sc'], lambda e: e.tensor_single_scalar(sc[:, 3:4], sc[:, 3:4], -1.0, ALU.add))
        kb.op('pool', [], ['m_cos'], lambda e: e.iota(cosT[:], [[1, S]], base=0, channel_multiplier=0,
                                                      allow_small_or_imprecise_dtypes=True))
        kb.op('dve', ['m_cos', 'm_sc'], ['m_cos'], lambda e: e.tensor_scalar(cosT[:], cosT[:], sc[:, 2:3], None, ALU.mult))
        TWO_PI = 2.0 * math.pi
        ki_ap = mq[:].bitcast(I32)

        def reduce_into(dst, dk, shift):
            for c4 in range(S // 512):
                cs = slice(c4 * 512, (c4 + 1) * 512)
                kb.op('dve', ['m_cos'], ['m_t1'],
                      lambda e: e.tensor_scalar(t1[:], cosT[:, cs], shift, 1.0 / TWO_PI, ALU.add, ALU.mult))
                kb.op('dve', ['m_t1'], ['m_q'], lambda e: e.tensor_copy(ki_ap, t1[:]))
                kb.op('dve', ['m_q'], ['m_t1'], lambda e: e.tensor_copy(t1[:], ki_ap))
                kb.op('dve', ['m_t1', 'm_cos'], ['m_t1'],
                      lambda e: e.scalar_tensor_tensor(t1[:], t1[:], -TWO_PI, cosT[:, cs], ALU.mult, ALU.add))
                kb.op('dve', ['m_t1'], ['m_t1'], lambda e: e.tensor_single_scalar(t1[:], t1[:], shift, ALU.add))
                kb.op('dve', ['m_t1'], ['m_k'],
                      lambda e: e.tensor_scalar(mk[:], t1[:], math.pi, -TWO_PI, ALU.is_gt, ALU.mult))
                kb.op('dve', ['m_t1', 'm_k'], ['m_t1'], lambda e: e.tensor_tensor(t1[:], t1[:], mk[:], ALU.add))
                kb.op('dve', ['m_t1'], ['m_k'],
                      lambda e: e.tensor_scalar(mk[:], t1[:], -math.pi, TWO_PI, ALU.is_lt, ALU.mult))
                kb.op('dve', ['m_t1', 'm_k'], [dk], lambda e: e.tensor_tensor(dst[:, cs], t1[:], mk[:], ALU.add))
        reduce_into(sinT, 'm_sin', 0.0)
        reduce_into(cosT, 'm_cos', 0.5 * math.pi)
        kb.op('act', ['m_sin'], ['m_sin'], lambda e: e.activation(sinT[:], sinT[:], AF.Sin))
        kb.op('act', ['m_cos'], ['m_cos'], lambda e: e.activation(cosT[:], cosT[:], AF.Sin))
        kb.op('dve', ['m_sin', 'm_sc'], ['m_sin'], lambda e: e.tensor_scalar(sinT[:], sinT[:], sc[:, 3:4], None, ALU.mult))
        for kk in range(2):
            kb.op('pool', [], ['m_t1'], lambda e: e.iota(t1[:, 0:256], [[1, 256]], base=-128 * kk, channel_multiplier=-1,
                                                        allow_small_or_imprecise_dtypes=True))
            kb.op('dve', ['m_t1'], ['m_cm'], lambda e: e.tensor_single_scalar(cm[:, kk, :], t1[:, 0:256], 0.0, ALU.is_ge))
        kb.op('dve', [], ['m_v1'], lambda e: e.memset(v1[:, :, 128:129], 1.0))
        QS = HD ** -0.5
        for h in range(NH):
            wh, wk = self.load_head_w(w_in, h, 4)
            kb.op('dve', [], ['m_km'], lambda e: e.memset(km[:], 0.0))
            for blk in range(S // 512):
                t0 = blk * 512
                tts = [blk * 4 + j for j in range(4)]
                xk = [('xT', tt) for tt in tts]
                bs = slice(t0, t0 + 512)
                for (pi, c0, dst, dk) in ((0, 0, mq, 'm_q'), (1, 128, mk, 'm_k')):
                    for kt in range(KT):
                        kb.op('pe', xk + [wk], ['pb%d' % pi],
                              lambda e: e.matmul(pb[pi][:], wh[:, kt, c0:c0 + 128], self.xT[:, kt, bs],
                                                 start=(kt == 0), stop=(kt == KT - 1)))
                    kb.op('act', ['pb%d' % pi], [dk], lambda e: e.copy(dst[:], pb[pi][:]))
                for (pi, src, sk, scale, dstb, dbk) in ((2, mq, 'm_q', QS, qT, 'm_qT'), (3, mk, 'm_k', 1.0, kT, 'm_kT')):
                    kb.op('pe', [sk, 'pswap'], ['pb%d' % pi],
                          lambda e: e.matmul(pb[pi][:], self.pswap[:], src[:], start=True, stop=True))
                    kb.op('dve', ['pb%d' % pi, 'm_sin'], ['m_t1'],
                          lambda e: e.tensor_tensor(t1[:], pb[pi][:], sinT[:, bs], ALU.mult))
                    kb.op('dve', [sk, 'm_cos'], [sk], lambda e: e.tensor_tensor(src[:], src[:], cosT[:, bs], ALU.mult))
                    if scale != 1.0:
                        kb.op('dve', [sk, 'm_t1'], [sk], lambda e: e.tensor_tensor(src[:], src[:], t1[:], ALU.add))
                        kb.op('dve', [sk], [sk], lambda e: e.tensor_single_scalar(src[:], src[:], scale, ALU.mult))
                        kb.op('act', [sk], [dbk], lambda e: e.copy(dstb[:, bs], src[:]))
                    else:
                        kb.op('dve', [sk, 'm_t1'], [sk], lambda e: e.tensor_tensor(src[:], src[:], t1[:], ALU.add))
                        kb.op('act', [sk], [dbk], lambda e: e.copy(dstb[:, bs], src[:]))
                kb.op('dve', ['m_k'], ['m_km'],
                      lambda e: e.tensor_reduce(km[:, 2 * blk:2 * blk + 2], mk[:].rearrange("p (b t) -> p b t", b=2), AX.X, ALU.add))
                kb.op('dve', ['m_km'], ['m_km'],
                      lambda e: e.tensor_single_scalar(km[:, 2 * blk:2 * blk + 2], km[:, 2 * blk:2 * blk + 2], 1.0 / 256, ALU.mult))
                for j in range(4):
                    tt = tts[j]
                    ts = slice(tt * 128, (tt + 1) * 128)
                    for kt in range(KT):
                        kb.op('pe', [('xT', tt), wk], ['pb4'],
                              lambda e: e.matmul(pb[4][:, 0:256], self.xT[:, kt, ts], wh[:, kt, 256:512],
                                                 start=(kt == 0), stop=(kt == KT - 1)))
                    kb.op('dve', ['pb4'], [('m_v1', tt)], lambda e: e.tensor_copy(v1[:, tt, 0:128], pb[4][:, 0:128]))
                    kb.op('act', ['pb4'], [('m_sg', tt)], lambda e: e.activation(sg[:, tt, :], pb[4][:, 128:256], AF.Silu))
                    m = tt // 2
                    kb.op('pe', ['m_q', 'm_km'], ['pb5'],
                          lambda e: e.matmul(pb[5][:, 0:8], qf[:, j * 128:(j + 1) * 128], km[:, 0:8], start=True, stop=True))
                    kb.op('dve', [], ['m_g8'], lambda e: e.memset(g8[:], -1e30))
                    if m > 0:
                        kb.op('dve', ['pb5'], ['m_g8'], lambda e: e.tensor_copy(g8[:, 0:m], pb[5][:, 0:m]))
                    kb.op('dve', ['m_g8'], ['m_mx8'], lambda e: e.max(mx8[:], g8[:]))
                    kb.op('dve', ['m_mx8'], ['m_mx8'], lambda e: e.tensor_single_scalar(mx8[:, 2:3], mx8[:, 2:3], -1e29, ALU.max))
                    kb.op('dve', ['m_g8', 'm_mx8'], [('m_sel', tt)],
                          lambda e: e.tensor_scalar(sel[:, tt, :], g8[:], mx8[:, 2:3], None, ALU.is_ge))
            pi_s = 0
            for m in range(S // 256):
                qs = slice(m * 256, (m + 1) * 256)
                order = [m] + list(range(m))
                for n in order:
                    for kk in range(2):
                        ks = slice(n * 256 + kk * 128, n * 256 + (kk + 1) * 128)
                        pi = pi_s % 2
                        pi_s += 1
                        kb.op('pe', ['m_kT', 'm_qT'], ['pb%d' % pi],
                              lambda e: e.matmul(pb[pi][:, 0:256], kT[:, ks], qT[:, qs], start=True, stop=True))
                        kb.op('act', ['pb%d' % pi], ['m_pT%d' % pi], lambda e: e.activation(pT[pi][:], pb[pi][:, 0:256], AF.Exp))
                        if n == m:
                            kb.op('dve', ['m_pT%d' % pi, 'm_cm'], ['m_pT%d' % pi],
                                  lambda e: e.tensor_tensor(pT[pi][:], pT[pi][:], cm[:, kk, :], ALU.mult))
                        for qt in range(2):
                            kb.op('pe', ['m_pT%d' % pi, ('m_v1', n * 2 + kk)], ['pb%d' % (2 + qt)],
                                  lambda e: e.matmul(pb[2 + qt][:, 0:129], pT[pi][:, qt * 128:(qt + 1) * 128],
                                                     v1[:, n * 2 + kk, 0:129], start=(kk == 0), stop=(kk == 1)))
                    for qt in range(2):
                        tt = m * 2 + qt
                        if n == m:
                            kb.op('dve', ['pb%d' % (2 + qt)], ['m_acc'], lambda e: e.tensor_copy(acc[:, qt, 0:129], pb[2 + qt][:, 0:129]))
                        else:
                            kb.op('dve', ['pb%d' % (2 + qt), 'm_acc', ('m_sel', tt)], ['m_acc'],
                                  lambda e: e.scalar_tensor_tensor(acc[:, qt, 0:129], pb[2 + qt][:, 0:129],
                                                                   sel[:, tt, n:n + 1], acc[:, qt, 0:129], ALU.mult, ALU.add))
                for qt in range(2):
                    tt = m * 2 + qt
                    kb.op('dve', ['m_acc'], ['m_sc'], lambda e: e.reciprocal(sc[:, 4:5], acc[:, qt, 128:129]))
                    kb.op('dve', ['m_acc', 'm_sc', ('m_sg', tt)], ['m_on'],
                          lambda e: e.scalar_tensor_tensor(mon[:], acc[:, qt, 0:128], sc[:, 4:5], sg[:, tt, :], ALU.mult, ALU.mult))
                    kb.op('pe', ['m_on', 'ident'], ['pb6'],
                          lambda e: e.matmul(pb[6][:, 0:128], mon[:], self.ident[:], start=True, stop=True))
                    kb.op('act', ['pb6'], [('oT', tt)],
                          lambda e: e.copy(self.oT[:, h, tt * 128:(tt + 1) * 128], pb[6][:, 0:128]))

    def s5(self, li):
        kb = self.kb
        A = self.A
        pb = self.pb
        p = "l%d_" % li
        w_in = A[p + "w_in"]
        NQ = 32
        TWO_PI = 2.0 * math.pi
        with ExitStack() as L:
            sbL = lambda n, sh, dt: kb.sb(n, sh, dt, L)
            pw = sbL("s_pw", [128, 50, NQ], F32)
            BT = sbL("s_BT", [128, 8, 2, 128], BF16)
            CT = sbL("s_CT", [128, 8, 2, 128], BF16)
            msk = sbL("s_msk", [128, 8], F32)
            dd = sbL("s_dd", [128, 8], F32)
            bg = sbL("s_bg", [128, 8], F32)
            kb.dma('sp', dd[:], A[p + "d"].rearrange("(c p) -> p c", p=128), [], ['s_dd'], 's_dd', allow_slow_non_contiguous=True)
            kb.dma('sp', bg[:], A[p + "b_glu"].rearrange("(c p) -> p c", p=128), [], ['s_bg'], 's_bg', allow_slow_non_contiguous=True)
            dv = lambda r, w, f: kb.op('dve', r, w, f)
            with ExitStack() as SS:
                sbS = lambda n, sh, dt: kb.sb(n, sh, dt, SS)
                ps = sbS("s_ps", [128, 24, NQ], F32)
                psi = sbS("s_psi", [128, NQ], I32)
                an = sbS("s_an", [64, 2, 64], F32)
                a2 = sbS("s_a2", [64, 2, 128], F32)
                ld = sbS("s_ld", [128, 64], F32)
                braw = sbS("s_braw", [128, 2, NQ, 16], F32)
                bb = sbS("s_bb", [128, 2, NQ, 16], F32)
                tb = sbS("s_tb", [128, NQ, 16], F32)
                cn = sbS("s_cn", [128, 2, 8, 64], F32)
                S4 = sbS("s_S4", [128, 2, 128], F32)
                M2 = sbS("s_M2", [128, 2, 128], F32)
                pif = sbS("s_pif", [128, 1], F32)
                AR, AI, DT, MAG, ANG, SIN, COS, DEN, NR, ZR, ZI, TF, TM, T1, T2 = range(15)
                P = lambda k: ps[:, k, :]
                kb.dma('sp', an[:, 0, :], A[p + "a_re"], [], ['s_an'], 's_an')
                kb.dma('sp', an[:, 1, :], A[p + "a_im"], [], ['s_an'], 's_an')
                kb.dma('sp', ld[:], A[p + "log_dt"].partition_broadcast(128), [], ['s_ld'], 's_ld')
                for ri, nm in ((0, "b_re"), (1, "b_im")):
                    v = A[p + nm].rearrange("(q two) p h -> two p q h", two=2)
                    for g2 in range(2):
                        kb.dma('sp', braw[g2 * 64:(g2 + 1) * 64, ri, :, :], v[g2], [], ['s_braw'], 's_braw')
                for ri, nm in ((0, "c_re"), (1, "c_im")):
                    kb.dma('sp', cn[:, ri, :, :], A[p + nm].rearrange("(c gg) h p -> (gg h) c p", gg=8), [], ['s_cn'], 's_cn')
                kb.op('pool', [], ['s_pif'], lambda e: e.iota(pif[:], [[0, 1]], base=0, channel_multiplier=1,
                                                              allow_small_or_imprecise_dtypes=True))
                dv([], ['s_msk'], lambda e: e.memset(msk[:], 0.0))
                for r in range(4):
                    dv(['s_pif'], ['s_msk'], lambda e: e.tensor_single_scalar(msk[:, 6:7], pif[:], float(32 * r), ALU.is_ge))
                    dv(['s_pif'], ['s_msk'], lambda e: e.tensor_single_scalar(msk[:, 7:8], pif[:], float(32 * r + 32), ALU.is_lt))
                    dv(['s_msk'], ['s_msk'], lambda e: e.tensor_tensor(msk[:, r:r + 1], msk[:, 6:7], msk[:, 7:8], ALU.mult))
                    dv(['s_pif'], ['s_msk'], lambda e: e.tensor_single_scalar(msk[:, 6:7], pif[:], float(32 * r + 16), ALU.is_ge))
                    dv(['s_msk'], ['s_msk'], lambda e: e.tensor_tensor(msk[:, 6:7], msk[:, 6:7], msk[:, 7:8], ALU.mult))
                    dv(['s_msk'], ['s_msk'], lambda e: e.tensor_tensor(msk[:, 5:6], msk[:, 5:6], msk[:, 6:7], ALU.add))
                dv(['s_msk'], ['s_msk'], lambda e: e.tensor_scalar(msk[:, 4:5], msk[:, 5:6], -1.0, 1.0, ALU.mult, ALU.add))
                dv(['s_an'], ['s_a2'], lambda e: e.tensor_copy(a2[:, :, 0:64], an[:]))
                dv(['s_an'], ['s_a2'], lambda e: e.tensor_copy(a2[:, :, 64:128], an[:]))
                for ri in range(2):
                    kb.op('pe', ['s_a2', 'identf'], ['pb0'],
                          lambda e: e.matmul(pb[0][:, ri * 64:(ri + 1) * 64], a2[:, ri, :], self.identf[0:64, 0:64],
                                             start=True, stop=True))
                for ri, dst in ((0, AR), (1, AI)):
                    v = pb[0][:, ri * 64:(ri + 1) * 64].rearrange("l (q two) -> l q two", two=2)
                    for g2 in range(2):
                        hs = slice(g2 * 64, (g2 + 1) * 64)
                        dv(['pb0'], ['s_ps'], lambda e: e.tensor_copy(ps[hs, dst, :], v[hs, :, g2]))
                v = ld[:].rearrange("l (q two) -> l q two", two=2)
                for g2 in range(2):
                    hs = slice(g2 * 64, (g2 + 1) * 64)
                    dv(['s_ld'], ['s_ps'], lambda e: e.tensor_copy(ps[hs, DT, :], v[hs, :, g2]))
                kb.op('act', ['s_ps'], ['s_ps'], lambda e: e.activation(P(DT), P(DT), AF.Exp))
                dv(['s_ps'], ['s_ps'], lambda e: e.tensor_tensor(P(MAG), P(AR), P(DT), ALU.mult))
                kb.op('act', ['s_ps'], ['s_ps'], lambda e: e.activation(P(MAG), P(MAG), AF.Exp))
                dv(['s_ps'], ['s_ps'], lambda e: e.tensor_tensor(P(ANG), P(AI), P(DT), ALU.mult))

                def sin_of(dst, shift):
                    dv(['s_ps'], ['s_ps'], lambda e: e.tensor_scalar(P(TF), P(ANG), shift, 1.0 / TWO_PI, ALU.add, ALU.mult))
                    dv(['s_ps'], ['s_psi'], lambda e: e.tensor_copy(psi[:], P(TF)))
                    dv(['s_psi'], ['s_ps'], lambda e: e.tensor_copy(P(TF), psi[:]))
                    dv(['s_ps'], ['s_ps'], lambda e: e.scalar_tensor_tensor(P(TF), P(TF), -TWO_PI, P(ANG), ALU.mult, ALU.add))
                    dv(['s_ps'], ['s_ps'], lambda e: e.tensor_single_scalar(P(TF), P(TF), shift, ALU.add))
                    dv(['s_ps'], ['s_ps'], lambda e: e.tensor_scalar(P(TM), P(TF), math.pi, -TWO_PI, ALU.is_gt, ALU.mult))
                    dv(['s_ps'], ['s_ps'], lambda e: e.tensor_tensor(P(TF), P(TF), P(TM), ALU.add))
                    dv(['s_ps'], ['s_ps'], lambda e: e.tensor_scalar(P(TM), P(TF), -math.pi, TWO_PI, ALU.is_lt, ALU.mult))
                    dv(['s_ps'], ['s_ps'], lambda e: e.tensor_tensor(P(TF), P(TF), P(TM), ALU.add))
                    kb.op('act', ['s_ps'], ['s_ps'], lambda e: e.activation(P(dst), P(TF), AF.Sin))
                sin_of(SIN, 0.0)
                sin_of(COS, 0.5 * math.pi)
                W = lambda k: pw[:, k, :]
                dv(['s_ps'], ['s_pw'], lambda e: e.tensor_tensor(W(0), P(MAG), P(COS), ALU.mult))
                dv(['s_ps'], ['s_pw'], lambda e: e.tensor_tensor(W(1), P(MAG), P(SIN), ALU.mult))
                dv(['s_pw'], ['s_pw'], lambda e: e.tensor_single_scalar(W(2), W(1), -1.0, ALU.mult))

                def cmul(d, a_, b_):
                    dv(['s_pw'], ['s_pw'], lambda e: e.tensor_tensor(W(48), W(a_), W(b_), ALU.mult))
                    dv(['s_pw'], ['s_pw'], lambda e: e.tensor_tensor(W(49), W(a_ + 1), W(b_ + 1), ALU.mult))
                    dv(['s_pw'], ['s_pw'], lambda e: e.tensor_tensor(W(d), W(48), W(49), ALU.subtract))
                    dv(['s_pw'], ['s_pw'], lambda e: e.tensor_tensor(W(48), W(a_), W(b_ + 1), ALU.mult))
                    dv(['s_pw'], ['s_pw'], lambda e: e.tensor_tensor(W(49), W(a_ + 1), W(b_), ALU.mult))
                    dv(['s_pw'], ['s_pw'], lambda e: e.tensor_tensor(W(d + 1), W(48), W(49), ALU.add))
                    dv(['s_pw'], ['s_pw'], lambda e: e.tensor_single_scalar(W(d + 2), W(d + 1), -1.0, ALU.mult))
                for j in range(1, 8):
                    cmul(3 * j, 3 * (j - 1), 0)
                for k in range(3):
                    dv(['s_pw'], ['s_pw'], lambda e: e.tensor_copy(W(24 + k), W(21 + k)))
                for k in range(1, 8):
                    cmul(24 + 3 * k, 24 + 3 * (k - 1), 24 + 3 * (k - 1))
                dv(['s_ps'], ['s_ps'], lambda e: e.tensor_tensor(P(DEN), P(AR), P(AR), ALU.mult))
                dv(['s_ps'], ['s_ps'], lambda e: e.tensor_tensor(P(T1), P(AI), P(AI), ALU.mult))
                dv(['s_ps'], ['s_ps'], lambda e: e.tensor_tensor(P(DEN), P(DEN), P(T1), ALU.add))
                dv(['s_ps'], ['s_ps'], lambda e: e.reciprocal(P(DEN), P(DEN)))
                dv(['s_pw'], ['s_ps'], lambda e: e.tensor_single_scalar(P(NR), W(0), -1.0, ALU.add))
                dv(['s_ps'], ['s_ps'], lambda e: e.tensor_tensor(P(T1), P(NR), P(AR), ALU.mult))
                dv(['s_ps', 's_pw'], ['s_ps'], lambda e: e.tensor_tensor(P(T2), W(1), P(AI), ALU.mult))
                dv(['s_ps'], ['s_ps'], lambda e: e.tensor_tensor(P(T1), P(T1), P(T2), ALU.add))
                dv(['s_ps'], ['s_ps'], lambda e: e.tensor_tensor(P(ZR), P(T1), P(DEN), ALU.mult))
                dv(['s_ps', 's_pw'], ['s_ps'], lambda e: e.tensor_tensor(P(T1), W(1), P(AR), ALU.mult))
                dv(['s_ps'], ['s_ps'], lambda e: e.tensor_tensor(P(T2), P(NR), P(AI), ALU.mult))
                dv(['s_ps'], ['s_ps'], lambda e: e.tensor_tensor(P(T1), P(T1), P(T2), ALU.subtract))
                dv(['s_ps'], ['s_ps'], lambda e: e.tensor_tensor(P(ZI), P(T1), P(DEN), ALU.mult))
                zb = lambda k: ps[:, k, :].unsqueeze(2).broadcast_to([128, NQ, 16])
                dv(['s_braw', 's_ps'], ['s_bb'], lambda e: e.tensor_tensor(bb[:, 0], braw[:, 0], zb(ZR), ALU.mult))
                dv(['s_braw', 's_ps'], ['s_tb'], lambda e: e.tensor_tensor(tb[:], braw[:, 1], zb(ZI), ALU.mult))
                dv(['s_bb', 's_tb'], ['s_bb'], lambda e: e.tensor_tensor(bb[:, 0], bb[:, 0], tb[:], ALU.subtract))
                dv(['s_braw', 's_ps'], ['s_bb'], lambda e: e.tensor_tensor(bb[:, 1], braw[:, 1], zb(ZR), ALU.mult))
                dv(['s_braw', 's_ps'], ['s_tb'], lambda e: e.tensor_tensor(tb[:], braw[:, 0], zb(ZI), ALU.mult))
                dv(['s_bb', 's_tb'], ['s_bb'], lambda e: e.tensor_tensor(bb[:, 1], bb[:, 1], tb[:], ALU.add))
                dv([], ['s_S4'], lambda e: e.memset(S4[:], 0.0))
                for c in range(8):
                    for ri in range(2):
                        v0 = S4[0:64, ri, :].rearrange("p (r x) -> p r x", x=32)
                        v1 = S4[64:128, ri, :].rearrange("p (r x) -> p r x", x=32)
                        dv(['s_bb'], ['s_S4'], lambda e: e.tensor_copy(v0[:, :, 0:16], bb[0:64, ri, 4 * c:4 * c + 4, :]))
                        dv(['s_bb'], ['s_S4'], lambda e: e.tensor_copy(v1[:, :, 16:32], bb[64:128, ri, 4 * c:4 * c + 4, :]))
                        kb.op('pe', ['s_S4', 'identf'], ['pb1'],
                              lambda e: e.matmul(pb[1][:, 0:128], S4[:, ri, :], self.identf[:], start=True, stop=True))
                        kb.op('act', ['pb1'], ['s_BT'], lambda e: e.copy(BT[:, c, ri, :], pb[1][:, 0:128]))
                        dv(['s_cn', 's_msk'], ['s_M2'],
                           lambda e: e.tensor_scalar(M2[:, ri, 0:64], cn[:, ri, c, :], msk[:, 4:5], None, ALU.mult))
                        dv(['s_cn', 's_msk'], ['s_M2'],
                           lambda e: e.tensor_scalar(M2[:, ri, 64:128], cn[:, ri, c, :], msk[:, 5:6], None, ALU.mult))
                        kb.op('pe', ['s_M2', 'identf'], ['pb2'],
                              lambda e: e.matmul(pb[2][:, 0:128], M2[:, ri, :], self.identf[:], start=True, stop=True))
                        kb.op('act', ['pb2'], ['s_CT'],
                              lambda e: e.activation(CT[:, c, ri, :], pb[2][:, 0:128], AF.Copy, scale=(1.0 if ri == 0 else -1.0)))
                kb.barrier()
            with ExitStack() as SC:
                sbC = lambda n, sh, dt: kb.sb(n, sh, dt, SC)
                Xr = sbC("s_Xr", [128, S], F32)
                Xi = sbC("s_Xi", [128, S], F32)
                Xb = sbC("s_Xb", [128, 2, 1024], BF16)
                yc = sbC("s_yc", [128, S], F32)
                ub = sbC("s_ub", [128, S], BF16)
                BTm = sbC("s_BTm", [128, 2, 2, 128], BF16)
                Ez = sbC("s_Ez", [128, 2, 260], F32)
                tk = Xb[:].bitcast(F32)
                W1 = lambda k, q: pw[:, k, q:q + 1]
                KXr = [('sXr', j) for j in range(8)]
                KXi = [('sXi', j) for j in range(8)]
                XK = KXr + KXi
                xrv = Xr[:].rearrange("p (j c) -> p j c", j=8)
                xiv = Xi[:].rearrange("p (j c) -> p j c", j=8)
                Xb4 = Xb[:].rearrange("p r (j c) -> p r j c", j=8)
                dv([], ['sEr', 'sEi'], lambda e: e.memset(Ez[:], 0.0))
                wu = [self.load_w(w_in[:, 0:512], 512), self.load_w(w_in[:, 512:1024], 512)]
                for c in range(8):
                    whu, wku = wu[c // 4]
                    cc = (c % 4) * 128
                    for b in range(4):
                        bs = slice(b * 512, (b + 1) * 512)
                        pi = b % 2
                        xk = [('xT', b * 4 + j) for j in range(4)]
                        for kt in range(KT):
                            kb.op('pe', xk + [wku], ['pb%d' % pi],
                                  lambda e: e.matmul(pb[pi][:], whu[:, kt, cc:cc + 128], self.xT[:, kt, bs],
                                                     start=(kt == 0), stop=(kt == KT - 1)))
                        kb.op('act', ['pb%d' % pi], [('s_ub', b)], lambda e: e.copy(ub[:, bs], pb[pi][:]))
                        dv(['pb%d' % pi, 's_dd'], [('s_yc', b)],
                           lambda e: e.tensor_scalar(yc[:, bs], pb[pi][:], dd[:, c:c + 1], None, ALU.mult))
                    for r in range(4):
                        q = 4 * c + r
                        qs_ = q % 2
                        for ri in range(2):
                            kb.op('act', ['s_BT', 's_msk'], [('s_BTm', qs_)],
                                  lambda e: e.activation(BTm[:, qs_, ri, :], BT[:, c, ri, :], AF.Copy, scale=msk[:, r:r + 1]))
                        for b in range(4):
                            bs = slice(b * 512, (b + 1) * 512)
                            ui = b % 2
                            for ri, Xv, KX in ((0, xrv, KXr), (1, xiv, KXi)):
                                pi = 2 + 2 * ui + ri
                                kb.op('pe', [('s_ub', b), ('s_BTm', qs_)], ['pb%d' % pi],
                                      lambda e: e.matmul(pb[pi][:], BTm[:, qs_, ri, :],
                                                         ub[:, bs].rearrange("p (c j) -> p j c", j=8), start=True, stop=True))
                                if ri == 0:
                                    kb.op('act', ['pb%d' % pi], KX,
                                          lambda e: e.copy(Xv[:, :, b * 64:(b + 1) * 64], pb[pi][:].rearrange("p (j c) -> p j c", j=8)))
                                else:
                                    dv(['pb%d' % pi], KX,
                                       lambda e: e.tensor_copy(Xv[:, :, b * 64:(b + 1) * 64], pb[pi][:].rearrange("p (j c) -> p j c", j=8)))
                        for j in range(1, 8):
                            rj, rjm, ij, ijm = ('sXr', j), ('sXr', j - 1), ('sXi', j), ('sXi', j - 1)
                            dv([rjm, rj, 's_pw'], [rj], lambda e: e.scalar_tensor_tensor(xrv[:, j, :], xrv[:, j - 1, :], W1(0, q), xrv[:, j, :], ALU.mult, ALU.add))
                            dv([ijm, ij, 's_pw'], [ij], lambda e: e.scalar_tensor_tensor(xiv[:, j, :], xiv[:, j - 1, :], W1(0, q), xiv[:, j, :], ALU.mult, ALU.add))
                            dv([ijm, rj, 's_pw'], [rj], lambda e: e.scalar_tensor_tensor(xrv[:, j, :], xiv[:, j - 1, :], W1(2, q), xrv[:, j, :], ALU.mult, ALU.add))
                            dv([rjm, ij, 's_pw'], [ij], lambda e: e.scalar_tensor_tensor(xiv[:, j, :], xrv[:, j - 1, :], W1(1, q), xiv[:, j, :], ALU.mult, ALU.add))
                        dv([('sXr', 7)], ['sEr'], lambda e: e.tensor_copy(Ez[:, 0, 1:257], xrv[:, 7, :]))
                        dv([('sXi', 7)], ['sEi'], lambda e: e.tensor_copy(Ez[:, 1, 1:257], xiv[:, 7, :]))
                        TR, TI = 'sXbr', 'sXbi'
                        for k in range(8):
                            sft = 1 << k
                            n = 256 - sft
                            kq = 24 + 3 * k
                            lo = slice(1, 1 + n)
                            hi = slice(1 + sft, 257)
                            dv(['sEr', 's_pw'], [TR], lambda e: e.tensor_scalar(tk[:, 0, 0:n], Ez[:, 0, lo], W1(kq, q), None, ALU.mult))
                            dv(['sEi', 's_pw'], [TI], lambda e: e.tensor_scalar(tk[:, 1, 0:n], Ez[:, 1, lo], W1(kq, q), None, ALU.mult))
                            dv(['sEi', 's_pw', TR], [TR], lambda e: e.scalar_tensor_tensor(tk[:, 0, 0:n], Ez[:, 1, lo], W1(kq + 2, q), tk[:, 0, 0:n], ALU.mult, ALU.add))
                            dv(['sEr', 's_pw', TI], [TI], lambda e: e.scalar_tensor_tensor(tk[:, 1, 0:n], Ez[:, 0, lo], W1(kq + 1, q), tk[:, 1, 0:n], ALU.mult, ALU.add))
                            dv(['sEr', TR], ['sEr'], lambda e: e.tensor_tensor(Ez[:, 0, hi], Ez[:, 0, hi], tk[:, 0, 0:n], ALU.add))
                            dv(['sEi', TI], ['sEi'], lambda e: e.tensor_tensor(Ez[:, 1, hi], Ez[:, 1, hi], tk[:, 1, 0:n], ALU.add))
                        for hf in range(2):
                            cs = slice(hf * 128, (hf + 1) * 128)
                            xbr = Xb[:, 0, :].rearrange("p (c j) -> p c j", j=8)
                            xbi = Xb[:, 1, :].rearrange("p (c j) -> p c j", j=8)
                            for j in range(8):
                                pj = 3 * j
                                rj, ij = ('sXr', j), ('sXi', j)
                                dv([rj, 'sEr', 's_pw'], [rj], lambda e: e.scalar_tensor_tensor(xrv[:, j, cs], Ez[:, 0, cs], W1(pj, q), xrv[:, j, cs], ALU.mult, ALU.add))
                                dv([ij, 'sEi', 's_pw'], [ij], lambda e: e.scalar_tensor_tensor(xiv[:, j, cs], Ez[:, 1, cs], W1(pj, q), xiv[:, j, cs], ALU.mult, ALU.add))
                                dv([rj, 'sEi', 's_pw'], ['sXbr'], lambda e: e.scalar_tensor_tensor(Xb4[:, 0, j, :], Ez[:, 1, cs], W1(pj + 2, q), xrv[:, j, cs], ALU.mult, ALU.add))
                                dv([ij, 'sEr', 's_pw'], ['sXbi'], lambda e: e.scalar_tensor_tensor(Xb4[:, 1, j, :], Ez[:, 0, cs], W1(pj + 1, q), xiv[:, j, cs], ALU.mult, ALU.add))
                            for bb_ in range(2):
                                b = 2 * hf + bb_
                                bs = slice(b * 512, (b + 1) * 512)
                                ls = slice(bb_ * 512, (bb_ + 1) * 512)
                                pi = 6 + bb_
                                for ri, xk_ in ((0, 'sXbr'), (1, 'sXbi')):
                                    kb.op('pe', [xk_, 's_CT'], ['pb%d' % pi],
                                          lambda e: e.matmul(pb[pi][:], CT[:, c, ri, :], Xb4[:, ri, :, bb_ * 64:(bb_ + 1) * 64],
                                                             start=(ri == 0), stop=(ri == 1)))
                                ycv = yc[:, bs].rearrange("p (c j) -> p c j", j=8)
                                dv(['pb%d' % pi, 's_msk', ('s_yc', b)], [('s_yc', b)],
                                   lambda e: e.scalar_tensor_tensor(ycv, pb[pi][:].rearrange("p (j c) -> p c j", j=8),
                                                                    msk[:, r:r + 1], ycv, ALU.mult, ALU.add))
                    YK = [('s_yc', b) for b in range(4)]
                    kb.op('act', YK, XK, lambda e: e.activation(Xr[:], yc[:], AF.Square))
                    dv(XK, XK, lambda e: e.tensor_scalar(Xr[:], Xr[:], 0.044715, 1.0, ALU.mult, ALU.add))
                    dv(XK + YK, XK, lambda e: e.tensor_tensor(Xr[:], Xr[:], yc[:], ALU.mult))
                    kb.op('act', XK, XK, lambda e: e.activation(Xi[:], Xr[:], AF.Sigmoid, scale=1.5957691216057308))
                    dv(XK + YK, [('oT', tt) for tt in range(NT)], lambda e: e.tensor_tensor(self.oT[:, c, :], yc[:], Xi[:], ALU.mult))
                kb.barrier()
            with ExitStack() as SG:
                sbG = lambda n, sh, dt: kb.sb(n, sh, dt, SG)
                wg = sbG("s_wg", [128, KT, D], BF16)
                to = sbG("s_to", [128, 8, 512], BF16)
                sgl = sbG("s_sgl", [128, 512], F32)
                sgt = sbG("s_sgt", [128, 512], F32)
                for r4 in range(4):
                    kb.dma('sp', self.wst1[:, :, :], A[p + "w_glu"][:, r4 * 256:(r4 + 1) * 256].rearrange("(kt p) c -> p kt c", p=128),
                           [], ['wst0'], 'wst0')
                    kb.op('pool', ['wst0'], ['s_wg'], lambda e: e.tensor_copy(wg[:, :, r4 * 256:(r4 + 1) * 256], self.wst1[:, :, :]))
                wgt = [self.load_w(w_in[:, 1024:1536], 512), self.load_w(w_in[:, 1536:2048], 512)]
                for b in range(4):
                    bs = slice(b * 512, (b + 1) * 512)
                    ok = [('oT', b * 4 + j) for j in range(4)]
                    xk = [('xT', b * 4 + j) for j in range(4)]
                    for c in range(8):
                        whg, wkg = wgt[c // 4]
                        cc = (c % 4) * 128
                        for kt in range(KT):
                            kb.op('pe', ok + ['s_wg'], ['pb0'],
                                  lambda e: e.matmul(pb[0][:], wg[:, kt, c * 128:(c + 1) * 128], self.oT[:, kt, bs],
                                                     start=(kt == 0), stop=(kt == KT - 1)))
                        for kt in range(KT):
                            kb.op('pe', xk + [wkg], ['pb1'],
                                  lambda e: e.matmul(pb[1][:], whg[:, kt, cc:cc + 128], self.xT[:, kt, bs],
                                                     start=(kt == 0), stop=(kt == KT - 1)))
                        kb.op('act', ['pb0', 's_bg'], ['s_sgl'],
                              lambda e: e.activation(sgl[:], pb[0][:], AF.Sigmoid, bias=bg[:, c:c + 1], scale=1.0))
                        kb.op('act', ['pb1'], ['s_sgt'], lambda e: e.activation(sgt[:], pb[1][:], AF.Silu))
                        dv(['s_sgl', 's_sgt'], ['s_sgl'], lambda e: e.tensor_tensor(sgl[:], sgl[:], sgt[:], ALU.mult))
                        dv(ok + ['s_sgl'], ['s_to'], lambda e: e.tensor_tensor(to[:, c, :], self.oT[:, c, bs], sgl[:], ALU.mult))
                    kb.op('act', ['s_to'], ok, lambda e: e.copy(self.oT[:, :, bs], to[:]))
                kb.barrier()


_CACHE = {}


def kernel(**inputs):
    nseq = 16 // N_CORES
    if 'nc' not in _CACHE:
        _CACHE['nc'] = build_program(nseq)
    nc = _CACHE['nc']
    x = np.ascontiguousarray(inputs["x"], dtype=np.float32)
    in_maps = []
    for c in range(N_CORES):
        m = {k: np.ascontiguousarray(v, dtype=np.float32) for k, v in inputs.items() if k != "x"}
        m["x"] = x[c * nseq:(c + 1) * nseq]
        in_maps.append(m)
    res = run_bass_kernel_spmd(nc, in_maps, core_ids=list(range(N_CORES)))
    return np.concatenate([r["out"] for r in res.results], axis=0)
```
